# Optimizing a Trainium2 kernel written in Bass

```python
import math
import jax, jax.numpy as jnp
from jax import lax
import numpy as np

D_MODEL = 1024
BATCH = 8
SEQ = 2048
DEPTH = 4

GRID_W = 64
ATTN_HEADS = 8
HEAD_DIM = 64
ATTN_WIDTH = ATTN_HEADS * HEAD_DIM
WIN_ROWS = 8
WIN_COLS = 16
Q_COLS = 16
BAND_COLS = 2 * WIN_COLS
CONV_WIDTH = D_MODEL - ATTN_WIDTH
CONV_K = 31
IN_WIDTH = 3 * ATTN_WIDTH + 2 * CONV_WIDTH
N_EXPERTS = 16
EC_CAPACITY_FACTOR = 2
EXPERT_FF = 2048
NORM_EPS = 1e-6

kernel_name = "hybrid_natten_conformer_ec_moe_encoder"


def rms_norm(x, g):
    x32 = x.astype(jnp.float32)
    y = x32 * lax.rsqrt(jnp.mean(x32 * x32, axis=-1, keepdims=True) + NORM_EPS)
    return (y * g.astype(jnp.float32)).astype(x.dtype)


def layer_norm(x, g, b):
    x32 = x.astype(jnp.float32)
    mu = jnp.mean(x32, axis=-1, keepdims=True)
    var = jnp.mean(jnp.square(x32 - mu), axis=-1, keepdims=True)
    y = (x32 - mu) * lax.rsqrt(var + NORM_EPS)
    return (y * g.astype(jnp.float32) + b.astype(jnp.float32)).astype(x.dtype)


def neighbourhood_attention(q, k, v, rpb):
    B, S, H, Dh = q.shape
    rows = S // GRID_W
    kh = min(WIN_ROWS, rows)
    n_cb = GRID_W // Q_COLS
    r = jnp.arange(rows)
    row_start = jnp.clip(r - kh // 2, 0, rows - kh)
    row_idx = row_start[:, None] + jnp.arange(kh)[None, :]
    cb = jnp.arange(n_cb)
    band_start = jnp.clip(cb * Q_COLS - WIN_COLS // 2, 0, GRID_W - BAND_COLS)
    col_idx = band_start[:, None] + jnp.arange(BAND_COLS)[None, :]
    qc = cb[:, None] * Q_COLS + jnp.arange(Q_COLS)[None, :]
    win_start = jnp.clip(qc - WIN_COLS // 2, 0, GRID_W - WIN_COLS)
    kc = col_idx[:, None, :]
    col_mask = (kc >= win_start[..., None]) & (kc < win_start[..., None] + WIN_COLS)
    row_off = row_idx - r[:, None] + (WIN_ROWS - 1)
    col_off = jnp.clip(kc - qc[..., None] + (WIN_COLS - 1), 0, 2 * WIN_COLS - 2)
    bias = rpb[:, row_off[:, None, None, :, None], col_off[None, :, :, None, :]]
    qg = q.reshape(B, rows, n_cb, Q_COLS, H, Dh)
    kgrid = k.reshape(B, rows, GRID_W, H, Dh)
    vgrid = v.reshape(B, rows, GRID_W, H, Dh)
    ri = row_idx[:, None, :, None]
    ci = col_idx[None, :, None, :]
    kg = kgrid[:, ri, ci]
    vg = vgrid[:, ri, ci]
    s = jnp.einsum('brcqhd,brckwhd->bhrcqkw', qg, kg,
                   preferred_element_type=jnp.float32) * (Dh ** -0.5)
    s = s + bias.astype(jnp.float32)[None]
    s = jnp.where(col_mask[:, :, None, :], s, -jnp.inf)
    p = jax.nn.softmax(s, axis=(-2, -1))
    o = jnp.einsum('bhrcqkw,brckwhd->brcqhd', p.astype(v.dtype), vg)
    return o.reshape(B, S, H * Dh)


def conformer_conv(a, gate, conv_w, conv_b, ln_g, ln_b):
    C = a.shape[-1]
    u = a * jax.nn.sigmoid(gate)
    dw = lax.conv_general_dilated(
        u, conv_w[:, None, :].astype(u.dtype), window_strides=(1,),
        padding=[(CONV_K // 2, CONV_K // 2)],
        dimension_numbers=('NWC', 'WIO', 'NWC'), feature_group_count=C)
    dw = dw + conv_b
    return jax.nn.silu(layer_norm(dw, ln_g, ln_b))


def expert_choice_ffn(h, w_router, w_gate, w_up, w_down):
    B, S, D = h.shape
    cap = EC_CAPACITY_FACTOR * S // N_EXPERTS
    logits = jnp.einsum('bsd,de->bse', h, w_router, preferred_element_type=jnp.float32)
    aff = jax.nn.softmax(logits, axis=-1)
    gates, idx = lax.top_k(jnp.swapaxes(aff, 1, 2), cap)
    bidx = jnp.arange(B)[:, None, None]
    xs = h[bidx, idx]
    hid = jax.nn.silu(jnp.einsum('becd,edf->becf', xs, w_gate)) * jnp.einsum('becd,edf->becf', xs, w_up)
    eo = jnp.einsum('becf,efd->becd', hid, w_down)
    return jnp.zeros_like(h).at[bidx, idx].add(gates[..., None].astype(h.dtype) * eo)


def setup_inputs(seed: int = 0) -> dict:
    key = jax.random.key(seed)
    ks = jax.random.split(key, 20)
    f32 = jnp.float32
    nrm = lambda k, shape, scale: jax.random.normal(k, shape, f32) * scale
    gain = lambda k, shape: 1.0 + 0.02 * jax.random.normal(k, shape, f32)
    return {
        "x": jax.random.normal(ks[0], (BATCH, SEQ, D_MODEL), f32),
        "norm1_g": gain(ks[1], (DEPTH, D_MODEL)),
        "w_in": nrm(ks[2], (DEPTH, D_MODEL, IN_WIDTH), D_MODEL ** -0.5),
        "rpb": nrm(ks[3], (DEPTH, ATTN_HEADS, 2 * WIN_ROWS - 1, 2 * WIN_COLS - 1), 0.02),
        "conv_w": nrm(ks[4], (DEPTH, CONV_K, CONV_WIDTH), CONV_K ** -0.5),
        "conv_b": nrm(ks[5], (DEPTH, CONV_WIDTH), 0.02),
        "conv_ln_g": gain(ks[6], (DEPTH, CONV_WIDTH)),
        "conv_ln_b": nrm(ks[7], (DEPTH, CONV_WIDTH), 0.02),
        "attn_out_g": gain(ks[8], (DEPTH, ATTN_WIDTH)),
        "conv_out_g": gain(ks[9], (DEPTH, CONV_WIDTH)),
        "w_out": nrm(ks[10], (DEPTH, D_MODEL, D_MODEL), (2 * DEPTH * D_MODEL) ** -0.5),
        "norm2_g": gain(ks[11], (DEPTH, D_MODEL)),
        "w_router": nrm(ks[12], (DEPTH, D_MODEL, N_EXPERTS), D_MODEL ** -0.5),
        "w_gate": nrm(ks[13], (DEPTH, N_EXPERTS, D_MODEL, EXPERT_FF), D_MODEL ** -0.5),
        "w_up": nrm(ks[14], (DEPTH, N_EXPERTS, D_MODEL, EXPERT_FF), D_MODEL ** -0.5),
        "w_down": nrm(ks[15], (DEPTH, N_EXPERTS, EXPERT_FF, D_MODEL), (2 * DEPTH * EXPERT_FF) ** -0.5),
        "final_g": gain(ks[16], (D_MODEL,)),
    }


def reference(x, norm1_g, w_in, rpb, conv_w, conv_b, conv_ln_g, conv_ln_b,
              attn_out_g, conv_out_g, w_out, norm2_g, w_router, w_gate, w_up,
              w_down, final_g):
    B, S, _ = x.shape
    splits = [ATTN_WIDTH, 2 * ATTN_WIDTH, 3 * ATTN_WIDTH, 3 * ATTN_WIDTH + CONV_WIDTH]
    for l in range(DEPTH):
        h = rms_norm(x, norm1_g[l])
        proj = jnp.einsum('bsd,de->bse', h, w_in[l])
        q, k, v, a, g = jnp.split(proj, splits, axis=-1)
        q = q.reshape(B, S, ATTN_HEADS, HEAD_DIM)
        k = k.reshape(B, S, ATTN_HEADS, HEAD_DIM)
        v = v.reshape(B, S, ATTN_HEADS, HEAD_DIM)
        attn = neighbourhood_attention(q, k, v, rpb[l])
        conv = conformer_conv(a, g, conv_w[l], conv_b[l], conv_ln_g[l], conv_ln_b[l])
        mixed = jnp.concatenate([rms_norm(attn, attn_out_g[l]), rms_norm(conv, conv_out_g[l])], axis=-1)
        x = x + jnp.einsum('bse,ed->bsd', mixed, w_out[l])
        h = rms_norm(x, norm2_g[l])
        x = x + expert_choice_ffn(h, w_router[l], w_gate[l], w_up[l], w_down[l])
    return rms_norm(x, final_g)
```

```python
import contextlib
import os
import numpy as np
import concourse.bass as bass
import concourse.mybir as mybir
from concourse.bass_utils import run_bass_kernel_spmd

F32 = mybir.dt.float32
BF16 = mybir.dt.bfloat16
AF = mybir.ActivationFunctionType
ALU = mybir.AluOpType
AX = mybir.AxisListType

D = 1024
S = 2048
DEPTH = 4
NT = 16
NE = 16
CAP = 256
FF = 2048
EPS = 1e-6
NEG = -30000.0
NBIS = 26
STOP = int(os.environ.get("KSTOP", "99"))


class _Stop(Exception):
    pass


def ckpt(k):
    if STOP == k:
        raise _Stop()


class Tr:
    __slots__ = ("w", "r", "psum")

    def __init__(self, deps=None, psum=False):
        self.w = None
        self.r = dict(deps) if deps else {}
        self.psum = psum


class V:
    __slots__ = ("ap", "trs")

    def __init__(self, ap, trs):
        self.ap = ap
        self.trs = trs if isinstance(trs, (list, tuple)) else [trs]


class Eng:
    def __init__(self, key):
        self.key = key
        self.cnt = 0
        self.seen = {}
        self.q = []


class FW:
    def __init__(self, nc, stack):
        self.nc = nc
        self.stack = stack
        self.sems = {}
        self.eng = {}
        for k in ("pe", "act", "dve", "pool", "sp"):
            self.eng[k] = Eng(k)
            self.sems[k] = stack.enter_context(nc.semaphore("s_" + k))
        self.dval = {}

    def dsem(self, key):
        if key not in self.sems:
            self.sems[key] = self.stack.enter_context(self.nc.semaphore("d_" + key))
            self.dval[key] = 0
        return key

    def _waits(self, E, needs):
        for k, val in needs.items():
            if val <= 0 or E.seen.get(k, 0) >= val:
                continue
            if k in self.eng and val > self.eng[k].cnt:
                raise RuntimeError(f"dangling wait: {E.key} needs {k}>={val} (cnt={self.eng[k].cnt})")
            E.seen[k] = val
            E.q.append(("wait", k, val))

    def _needs(self, E, reads, writes):
        needs = {}

        def need(k, v, war=False):
            if k == E.key and (E.key == "pe" or war):
                return
            if v > needs.get(k, 0):
                needs[k] = v

        for v_ in reads:
            for t in v_.trs:
                if t.w:
                    need(*t.w)
                if t.psum:
                    for k, val in t.r.items():
                        need(k, val, war=True)
        for v_ in writes:
            for t in v_.trs:
                if t.w:
                    need(*t.w)
                for k, val in t.r.items():
                    need(k, val, war=True)
        return needs

    def op(self, en, fn, reads=(), writes=(), inc=True):
        E = self.eng[en]
        self._waits(E, self._needs(E, reads, writes))
        if inc:
            E.cnt += 1
            mark = E.cnt
        else:
            mark = E.cnt + 1
        E.q.append(("op", fn, inc))
        for v_ in reads:
            for t in v_.trs:
                if t.r.get(E.key, 0) < mark:
                    t.r[E.key] = mark
        for v_ in writes:
            for t in v_.trs:
                t.w = (E.key, mark)
                t.r = {}

    def dma(self, en, fn, reads, writes, dsem):
        E = self.eng[en]
        self.dsem(dsem)
        needs = self._needs(E, reads, writes)
        if self.dval[dsem] > needs.get(dsem, 0):
            needs[dsem] = self.dval[dsem]
        self._waits(E, needs)
        self.dval[dsem] += 16
        val = self.dval[dsem]
        E.q.append(("dma", fn, dsem))
        for v_ in reads:
            for t in v_.trs:
                t.r[dsem] = val
        for v_ in writes:
            for t in v_.trs:
                t.w = (dsem, val)
                t.r = {}

    def final_wait(self, en, keys):
        self._waits(self.eng[en], {k: self.dval[k] for k in keys})

    def emit(self):
        for E in self.eng.values():
            assert E.cnt < 60000, (E.key, E.cnt)
        for k, v in self.dval.items():
            assert v < 60000, (k, v)
        with self.nc.Block() as block:
            def run(E):
                def body(e):
                    for it in E.q:
                        if it[0] == "wait":
                            e.wait_ge(self.sems[it[1]], it[2])
                        elif it[0] == "op":
                            ins = it[1](e)
                            if it[2]:
                                ins.then_inc(self.sems[E.key], 1)
                        else:
                            it[1](e).then_inc(self.sems[it[2]], 16)
                return body
            block.sync(run(self.eng["sp"]))
            block.tensor(run(self.eng["pe"]))
            block.scalar(run(self.eng["act"]))
            block.vector(run(self.eng["dve"]))
            block.gpsimd(run(self.eng["pool"]))


def deps_of(trs):
    d = {}
    for t in trs:
        if t.w and t.w[1] > d.get(t.w[0], 0):
            d[t.w[0]] = t.w[1]
        for k, v in t.r.items():
            if v > d.get(k, 0):
                d[k] = v
    return d


class Region:
    def __init__(self, nc, stack, name, nbytes):
        self.t = stack.enter_context(nc.sbuf_tensor(name, [128, nbytes // 2], BF16))
        self.nbytes = nbytes
        self.live = []
        self.base = {}
        self.off = 0

    def new_phase(self):
        self.base = deps_of(self.live + [Tr(self.base)])
        self.live = []
        self.off = 0

    def carve(self, shape, dtype, ntr=1, at=None):
        es = 4 if dtype == F32 else 2
        n = int(np.prod(shape[1:]))
        nb = n * es
        off = self.off if at is None else at
        off = (off + 63) // 64 * 64
        assert off + nb <= self.nbytes, (self.t, off, nb, self.nbytes)
        if at is None:
            self.off = off + nb
        ap = self.t[0:shape[0], off // 2: (off + nb) // 2]
        if dtype != BF16:
            ap = ap.bitcast(dtype)
        if len(shape) == 3:
            ap = ap.rearrange("p (a b) -> p a b", b=shape[2])
        elif len(shape) == 4:
            ap = ap.rearrange("p (a b c) -> p a b c", b=shape[2], c=shape[3])
        trs = [Tr(self.base) for _ in range(ntr)]
        self.live += trs
        return ap, trs


def build(nl=DEPTH):
    nc = bass.Bass("TRN2", target_bir_lowering=False)

    def din(name, shape, dt=F32):
        return nc.dram_tensor(name, shape, dt, kind="ExternalInput").ap()

    x_d = din("x", [S, D])
    n1g_d = din("norm1_g", [DEPTH, D])
    n2g_d = din("norm2_g", [DEPTH, D])
    fg_d = din("final_g", [1, D])
    aog_d = din("attn_out_g", [DEPTH, 512])
    win_d = din("w_in", [DEPTH, D, 2560])
    wout_d = din("w_out", [DEPTH, D, D])
    wr_d = din("w_router", [DEPTH, D, NE])
    wg_d = din("w_gate", [DEPTH * NE, D, FF])
    wu_d = din("w_up", [DEPTH * NE, D, FF])
    wd_d = din("w_down", [DEPTH * NE, FF, D])
    cw_d = din("cw", [DEPTH, 128, 124])
    cvec_d = din("cvec", [DEPTH, 128, 16])
    eb_d = din("ebias", [DEPTH * 8, 128, 3200])
    cst_d = din("consts", [128, 704])
    out_d = nc.dram_tensor("out", [S, D], F32, kind="ExternalOutput").ap()
    h2_d = nc.dram_tensor("h2_scr", [S, D], BF16, kind="Internal").ap()

    with contextlib.ExitStack() as st:
        fw = FW(nc, st)
        sb = lambda n, s, d: st.enter_context(nc.sbuf_tensor(n, s, d))
        NOTR = []

        x_sb = sb("x_sb", [128, NT, D], F32)
        x_tr = [[Tr(), Tr()] for _ in range(NT)]
        xflat = lambda a, b: [t_ for pair in x_tr[a:b] for t_ in pair]
        RA = Region(nc, st, "RA", 32768)
        RS = Region(nc, st, "RS", 61440)
        RW = Region(nc, st, "RW", 36864)
        cst = sb("cst", [128, 704], F32); cst_tr = Tr()
        ident_f = cst[:, 0:128]
        iota_c = cst[:, 256:512]
        ident_b = sb("ident_b", [128, 128], BF16)
        tri_b = sb("tri_b", [128, 128], BF16)
        ones_b = sb("ones_b", [128, 128], BF16)
        onesd_b = sb("onesd_b", [128, 128], BF16)
        cb_tr = Tr()
        tokc_b = sb("tokc_b", [128, NT, 2], BF16)
        modm_b = sb("modm_b", [128, 128], BF16)
        idx_sb = sb("idx_sb", [128, 4], mybir.dt.int32); idx_tr = [Tr(), Tr()]
        H2D = Tr()
        idxf_sb = sb("idxf_sb", [128, 8], F32); idxf_tr = Tr()
        g_bc = sb("g_bc", [128, D], F32); g_tr = Tr()
        aog_bc = sb("aog_bc", [128, 512], F32); aog_tr = Tr()
        cw_sb = sb("cw_sb", [128, 124], F32); cw_tr = Tr()
        cvec_sb = sb("cvec_sb", [128, 16], F32); cvec_tr = Tr()
        wr_sb = sb("wr_sb", [128, 8, NE], BF16); wr_tr = Tr()
        ss_sb = sb("ss_sb", [128, NT], F32); ss_tr = Tr()
        rstd_sb = sb("rstd_sb", [128, NT], F32); rstd_tr = Tr()
        sm_sb = sb("sm_sb", [128, 64], F32); sm_tr = Tr()
        junk_sb = sb("junk_sb", [128, D], BF16); junk_tr = Tr()

        banks = [st.enter_context(nc.psum_tensor(f"bank{i}", [128, 512], F32)) for i in range(8)]
        b_tr = [Tr(psum=True) for _ in range(8)]

        def bank(i):
            return banks[i]

        def bankb(i):
            return banks[i][:].bitcast(BF16)

        rr = {"ev": 0}

        def evac_eng():
            rr["ev"] ^= 1
            return "act" if rr["ev"] else "dve"

        def copy(en, out, in_, reads, writes):
            if en == "act":
                fw.op("act", lambda e: e.copy(out=out, in_=in_), reads, writes)
            else:
                fw.op(en, lambda e: e.tensor_copy(out=out, in_=in_), reads, writes)

        fw.dma("sp", lambda e: e.dma_start(out=cst[:], in_=cst_d[:, :]), [], [V(None, cst_tr)], "ld_c")
        for t4 in range(4):
            fw.dma("sp", lambda e, t4=t4: e.dma_start(
                out=x_sb[:, t4 * 4:(t4 + 1) * 4, :],
                in_=x_d[t4 * 512:(t4 + 1) * 512, :].rearrange("(t p) d -> p t d", p=128)),
                [], [V(None, xflat(t4 * 4, (t4 + 1) * 4))], f"ld_x{t4}")
        fw.op("dve", lambda e: e.tensor_copy(out=ident_b[:], in_=cst[:, 0:128]), [V(None, cst_tr)], [V(None, cb_tr)])
        fw.op("dve", lambda e: e.tensor_copy(out=tri_b[:], in_=cst[:, 128:256]), [V(None, cst_tr)], [V(None, cb_tr)])
        fw.op("dve", lambda e: e.memset(ones_b[:], 1.0), [], [V(None, cb_tr)])
        fw.op("dve", lambda e: e.tensor_copy(out=modm_b[:], in_=cst[:, 576:704]), [V(None, cst_tr)], [V(None, cb_tr)])
        fw.op("dve", lambda e: e.tensor_copy(out=tokc_b[:].rearrange("p t k -> p (t k)"), in_=cst[:, 512:544]), [V(None, cst_tr)], [V(None, cb_tr)])
        fw.op("dve", lambda e: e.memset(onesd_b[:], 1.0 / 512.0), [], [V(None, cb_tr)])
        CB = V(None, cb_tr)
        CST = V(None, cst_tr)

        def load_gain(src_row_ap):
            fw.dma("sp", lambda e: e.dma_start(out=g_bc[:], in_=src_row_ap.partition_broadcast(128)),
                   [], [V(None, g_tr)], "ld_g")

        def rms_stats(width, get_in, n, in_reads):
            for i in range(n):
                fw.op("act", lambda e, i=i: e.activation(out=junk_sb[:, 0:width], in_=get_in(i), func=AF.Square,
                                                        accum_out=ss_sb[:, i:i + 1]),
                      [in_reads(i)], [V(None, junk_tr), V(None, ss_tr)])
            fw.op("act", lambda e: e.activation(out=rstd_sb[:, 0:n], in_=ss_sb[:, 0:n], func=AF.Ln, bias=EPS, scale=1.0 / width),
                  [V(None, ss_tr)], [V(None, rstd_tr)])
            fw.op("act", lambda e: e.activation(out=rstd_sb[:, 0:n], in_=rstd_sb[:, 0:n], func=AF.Exp, scale=-0.5),
                  [V(None, rstd_tr)], [V(None, rstd_tr)])

        for l in range(nl if STOP == 99 else 1):
          try:
            load_gain(n1g_d[l])
            fw.dma("sp", lambda e, l=l: e.dma_start(out=aog_bc[:], in_=aog_d[l].partition_broadcast(128)), [], [V(None, aog_tr)], "ld_p")
            fw.dma("sp", lambda e, l=l: e.dma_start(out=cw_sb[:], in_=cw_d[l]), [], [V(None, cw_tr)], "ld_p")
            fw.dma("sp", lambda e, l=l: e.dma_start(out=cvec_sb[:], in_=cvec_d[l]), [], [V(None, cvec_tr)], "ld_p")
            fw.dma("pool", lambda e, l=l: e.dma_start(out=wr_sb[:], in_=wr_d[l].rearrange("(c p) e -> p c e", p=128)),
                   [], [V(None, wr_tr)], "ld_wr")

            RA.new_phase(); RS.new_phase(); RW.new_phase()
            hT, hT_tr = RA.carve([128, 8, S], BF16, ntr=1)
            HT = V(None, hT_tr)
            htile, htile_tr = [], []
            for i in range(2):
                a_, t_ = RS.carve([128, D], BF16)
                htile.append(a_); htile_tr.append(t_[0])
            wslot, wslot_tr = [], []
            for i in range(3):
                a_, t_ = RW.carve([128, 6144], BF16)
                wslot.append(a_); wslot_tr.append(t_[0])
            wq = {"n": 0}

            def wload(fns):
                si = wq["n"] % 3
                wq["n"] += 1
                for f in fns:
                    o, i_ = f(wslot[si])
                    fw.dma("pool", lambda e, o=o, i_=i_: e.dma_start(out=o, in_=i_), [], [V(None, wslot_tr[si])], f"ld_w{si}")
                return si

            def win_piece(cb, l=l):
                return [lambda sl: (sl[:, 0:4096].rearrange("p (c f) -> p c f", f=512),
                                    win_d[l][:, cb * 512:(cb + 1) * 512].rearrange("(c p) f -> p c f", p=128))]

            def wout_piece(half, l=l):
                return [lambda sl: (sl[:, 0:4096].rearrange("p (c f) -> p c f", f=1024),
                                    wout_d[l][half * 512:(half + 1) * 512, :].rearrange("(c p) f -> p c f", p=128))]

            s_g = wload(win_piece(4))
            s_a = wload(win_piece(3))

            rms_stats(D, lambda i: x_sb[:, i, :], NT, lambda i: V(None, x_tr[i]))
            for stt in range(NT):
                hb = stt % 2
                fw.op("dve", lambda e, stt=stt, hb=hb: e.scalar_tensor_tensor(
                    out=htile[hb], in0=x_sb[:, stt, :], scalar=rstd_sb[:, stt:stt + 1], in1=g_bc[:],
                    op0=ALU.mult, op1=ALU.mult),
                    [V(None, x_tr[stt]), V(None, rstd_tr), V(None, g_tr)], [V(None, htile_tr[hb])])
                bk = stt % 2
                for dc in range(8):
                    fw.op("pe", lambda e, dc=dc, hb=hb, bk=bk: e.transpose(
                        out=bankb(bk)[:, dc * 128:(dc + 1) * 128], in_=htile[hb][:, dc * 128:(dc + 1) * 128], identity=ident_b[:]),
                        [V(None, htile_tr[hb]), CB], [V(None, b_tr[bk])], inc=(dc == 7))
                copy(evac_eng(), hT[:, :, stt * 128:(stt + 1) * 128], bankb(bk).rearrange("p (c s) -> p c s", s=128),
                     [V(None, b_tr[bk])], [HT])

            ckpt(1)
            RS.new_phase()
            upad, upad_tr = RS.carve([128, 4, S + 30], BF16)
            UP = V(None, upad_tr)
            diag, diag_tr = RS.carve([128, 31, 128], BF16)
            DG = V(None, diag_tr)
            cvn, cvn_tr = RS.carve([128, 4, S], BF16, ntr=4)
            zbuf, z_tr = RS.carve([128, 4, 512], F32, ntr=4)
            t1, t1_tr = RS.carve([128, 512], F32)
            t2, t2_tr = RS.carve([128, 512], F32)
            sqb, sqb_tr = RS.carve([128, 512], BF16)
            T1 = V(None, t1_tr); T2 = V(None, t2_tr); SQB = V(None, sqb_tr)
            fw.op("dve", lambda e: e.memset(upad[:, :, 0:15], 0.0), [], [UP])
            fw.op("dve", lambda e: e.memset(upad[:, :, S + 15:S + 30], 0.0), [], [UP])

            bkc = {"n": 0}

            def nbank(lo=0, hi=8):
                b = lo + bkc["n"] % (hi - lo)
                bkc["n"] += 1
                return b

            def proj_fm(slot, evac):
                for fc in range(4):
                    for sblk in range(4):
                        bk = nbank()
                        for dc in range(8):
                            fw.op("pe", lambda e, fc=fc, sblk=sblk, dc=dc, bk=bk, slot=slot: e.matmul(
                                bank(bk)[:, :], lhsT=wslot[slot][:, dc * 512 + fc * 128: dc * 512 + (fc + 1) * 128],
                                rhs=hT[:, dc, sblk * 512:(sblk + 1) * 512], start=(dc == 0), stop=(dc == 7)),
                                [V(None, wslot_tr[slot]), HT], [V(None, b_tr[bk])], inc=(dc == 7))
                        evac(fc, sblk, bk)

            proj_fm(s_g, lambda fc, sblk, bk: fw.op("act", lambda e: e.activation(
                out=upad[:, fc, 15 + sblk * 512: 15 + (sblk + 1) * 512], in_=bank(bk)[:, :], func=AF.Sigmoid),
                [V(None, b_tr[bk])], [UP]))
            s_wo2 = wload(wout_piece(1))
            proj_fm(s_a, lambda fc, sblk, bk: fw.op("dve", lambda e: e.tensor_tensor(
                out=upad[:, fc, 15 + sblk * 512: 15 + (sblk + 1) * 512], in0=bank(bk)[:, :],
                in1=upad[:, fc, 15 + sblk * 512: 15 + (sblk + 1) * 512], op=ALU.mult),
                [V(None, b_tr[bk]), UP], [UP]))
            s_q = wload(win_piece(0))
            s_k = wload(win_piece(1))

            ckpt(2)
            for c in range(4):
                fw.op("dve", lambda e, c=c: e.tensor_tensor(
                    out=diag, in0=ident_b[:].unsqueeze(1).to_broadcast([128, 31, 128]),
                    in1=cw_sb[:, c * 31:(c + 1) * 31].unsqueeze(2).to_broadcast([128, 31, 128]), op=ALU.mult),
                    [CB, V(None, cw_tr)], [DG])
                for sblk in range(4):
                    bk = nbank()
                    for j in range(31):
                        fw.op("pe", lambda e, c=c, j=j, sblk=sblk, bk=bk: e.matmul(
                            bank(bk)[:, :], lhsT=diag[:, j, :], rhs=upad[:, c, sblk * 512 + j: sblk * 512 + j + 512],
                            start=(j == 0), stop=(j == 30)),
                            [DG, UP], [V(None, b_tr[bk])], inc=(j == 30))
                    fw.op("act", lambda e, c=c, sblk=sblk, bk=bk: e.activation(
                        out=cvn[:, c, sblk * 512:(sblk + 1) * 512], in_=bank(bk)[:, :], func=AF.Identity,
                        bias=cvec_sb[:, c:c + 1]),
                        [V(None, b_tr[bk]), V(None, cvec_tr)], [V(None, cvn_tr[c])])

            ckpt(3)
            for sblk in range(4):
                sl = slice(sblk * 512, (sblk + 1) * 512)
                bm, be, b3 = nbank(), nbank(), nbank()
                for c in range(4):
                    fw.op("pe", lambda e, c=c, bm=bm, sl=sl: e.matmul(bank(bm)[:, :], lhsT=onesd_b[:], rhs=cvn[:, c, sl],
                                                                start=(c == 0), stop=(c == 3)),
                          [CB, V(None, cvn_tr[c])], [V(None, b_tr[bm])], inc=(c == 3))
                for c in range(4):
                    fw.op("act", lambda e, c=c, sl=sl: e.activation(out=sqb, in_=cvn[:, c, sl], func=AF.Square),
                          [V(None, cvn_tr[c])], [SQB])
                    fw.op("pe", lambda e, c=c, be=be: e.matmul(bank(be)[:, :], lhsT=onesd_b[:], rhs=sqb,
                                                                start=(c == 0), stop=(c == 3)),
                          [CB, SQB], [V(None, b_tr[be])], inc=True)
                fw.op("dve", lambda e, bm=bm: e.tensor_tensor(out=t1, in0=bank(bm)[:, :], in1=bank(bm)[:, :], op=ALU.mult)
                      if False else e.tensor_copy(out=t1, in_=bank(bm)[:, :]), [V(None, b_tr[bm])], [T1])
                fw.op("dve", lambda e: e.tensor_tensor(out=t2, in0=t1, in1=t1, op=ALU.mult), [T1], [T2])
                fw.op("dve", lambda e, be=be: e.tensor_tensor(out=t2, in0=bank(be)[:, :], in1=t2, op=ALU.subtract),
                      [V(None, b_tr[be]), T2], [T2])
                fw.op("act", lambda e: e.activation(out=t2, in_=t2, func=AF.Ln, bias=EPS), [T2], [T2])
                fw.op("act", lambda e: e.activation(out=t2, in_=t2, func=AF.Exp, scale=-0.5), [T2], [T2])
                for c in range(4):
                    ZC = V(None, z_tr[c])
                    fw.op("dve", lambda e, c=c, sl=sl: e.tensor_tensor(out=zbuf[:, c, :], in0=cvn[:, c, sl], in1=t1, op=ALU.subtract),
                          [V(None, cvn_tr[c]), T1], [ZC])
                    fw.op("dve", lambda e, c=c: e.tensor_tensor(out=zbuf[:, c, :], in0=zbuf[:, c, :], in1=t2, op=ALU.mult),
                          [ZC, T2], [ZC])
                    fw.op("act", lambda e, c=c: e.activation(out=zbuf[:, c, :], in_=zbuf[:, c, :], func=AF.Silu,
                                                            bias=cvec_sb[:, 8 + c:9 + c], scale=cvec_sb[:, 4 + c:5 + c]),
                          [ZC, V(None, cvec_tr)], [ZC])
                    fw.op("act", lambda e, c=c: e.activation(out=sqb, in_=zbuf[:, c, :], func=AF.Square), [ZC], [SQB])
                    fw.op("pe", lambda e, c=c, b3=b3: e.matmul(bank(b3)[:, :], lhsT=onesd_b[:], rhs=sqb,
                                                                start=(c == 0), stop=(c == 3)),
                          [CB, SQB], [V(None, b_tr[b3])], inc=True)
                fw.op("act", lambda e, b3=b3: e.activation(out=t2, in_=bank(b3)[:, :], func=AF.Ln, bias=EPS), [V(None, b_tr[b3])], [T2])
                fw.op("act", lambda e: e.activation(out=t2, in_=t2, func=AF.Exp, scale=-0.5), [T2], [T2])
                for c in range(4):
                    fw.op("dve", lambda e, c=c, sl=sl: e.scalar_tensor_tensor(
                        out=cvn[:, c, sl], in0=zbuf[:, c, :], scalar=cvec_sb[:, 12 + c:13 + c], in1=t2, op0=ALU.mult, op1=ALU.mult),
                        [V(None, z_tr[c]), V(None, cvec_tr), T2], [V(None, cvn_tr[c])])

            def out_proj(slot, srcT, src_reads):
                for stt in range(NT):
                    for dh in range(2):
                        bk = nbank()
                        for c in range(4):
                            fw.op("pe", lambda e, stt=stt, dh=dh, c=c, bk=bk: e.matmul(
                                bank(bk)[:, :], lhsT=srcT[:, c, stt * 128:(stt + 1) * 128],
                                rhs=wslot[slot][:, c * 1024 + dh * 512: c * 1024 + (dh + 1) * 512],
                                start=(c == 0), stop=(c == 3)),
                                [src_reads(c), V(None, wslot_tr[slot])], [V(None, b_tr[bk])], inc=(c == 3))
                        fw.op("dve", lambda e, stt=stt, dh=dh, bk=bk: e.tensor_tensor(
                            out=x_sb[:, stt, dh * 512:(dh + 1) * 512], in0=bank(bk)[:, :],
                            in1=x_sb[:, stt, dh * 512:(dh + 1) * 512], op=ALU.add),
                            [V(None, b_tr[bk]), V(None, x_tr[stt][dh])], [V(None, x_tr[stt][dh])])

            ckpt(4)
            out_proj(s_wo2, cvn, lambda c: V(None, cvn_tr[c]))
            ckpt(5)
            s_v = wload(win_piece(2))

            RS.new_phase()
            qT, qT_tr = RS.carve([128, 4, S], BF16)
            kT, kT_tr = RS.carve([128, 4, S], BF16)
            v1, v1_tr = RS.carve([128, NT, 8, 65], BF16)
            QT = V(None, qT_tr); KT = V(None, kT_tr); V1 = V(None, v1_tr)
            ebuf, ebuf_tr, ptb, ptb_tr = [], [], [], []
            for i in range(3):
                a_, t_ = RS.carve([128, 640], BF16); ebuf.append(a_); ebuf_tr.append(t_[0])
                a_, t_ = RS.carve([128, 640], BF16); ptb.append(a_); ptb_tr.append(t_[0])
            anb, anb_tr = RS.carve([128, 512], BF16)
            ANB = V(None, anb_tr)
            fw.op("dve", lambda e: e.memset(v1[:, :, :, 64:65], 1.0), [], [V1])

            proj_fm(s_q, lambda fc, sblk, bk: fw.op("act", lambda e: e.mul(
                out=qT[:, fc, sblk * 512:(sblk + 1) * 512], in_=bank(bk)[:, :], mul=0.125),
                [V(None, b_tr[bk])], [QT]))
            s_wo1 = wload(wout_piece(0))
            proj_fm(s_k, lambda fc, sblk, bk: fw.op("dve", lambda e: e.tensor_copy(
                out=kT[:, fc, sblk * 512:(sblk + 1) * 512], in_=bank(bk)[:, :]),
                [V(None, b_tr[bk])], [KT]))
            for stt in range(NT):
                bk = nbank()
                for dc in range(8):
                    fw.op("pe", lambda e, stt=stt, dc=dc, bk=bk: e.matmul(
                        bank(bk)[:, :], lhsT=hT[:, dc, stt * 128:(stt + 1) * 128], rhs=wslot[s_v][:, dc * 512:(dc + 1) * 512],
                        start=(dc == 0), stop=(dc == 7)),
                        [HT, V(None, wslot_tr[s_v])], [V(None, b_tr[bk])], inc=(dc == 7))
                copy(evac_eng(), v1[:, stt, :, 0:64], bank(bk)[:, :].rearrange("p (h d) -> p h d", d=64),
                     [V(None, b_tr[bk])], [V1])

            ckpt(6)
            RA.new_phase()
            anT, anT_tr = RA.carve([128, 4, S], BF16)
            ANT = V(None, anT_tr)
            ebias, ebias_tr = [], []
            for i in range(2):
                a_, t_ = RA.carve([128, 5, 640], BF16); ebias.append(a_); ebias_tr.append(t_[0])

            def load_ebias(h, l=l):
                i = h % 2
                fw.dma("pool", lambda e: e.dma_start(out=ebias[i], in_=eb_d[l * 8 + h].rearrange("p (t k) -> p t k", k=640)),
                       [], [V(None, ebias_tr[i])], f"ld_eb{i}")

            def exp_ebias(h):
                i = h % 2
                fw.op("act", lambda e: e.activation(out=ebias[i], in_=ebias[i], func=AF.Exp),
                      [V(None, ebias_tr[i])], [V(None, ebias_tr[i])])

            _ad = deps_of([wslot_tr[1], wslot_tr[2]])
            aout = RW.t[:, 6144:6144 + 8192].rearrange("p (t f) -> p t f", f=512)
            aout_tr = [Tr(_ad) for _ in range(NT)]
            RW.live += aout_tr
            load_ebias(0)
            load_ebias(1)
            exp_ebias(0)
            exp_ebias(1)
            pend_exp = None
            SA, SB_ = 0, 2
            step = 0
            pendq = []
            for h in range(8):
                hc, hp = h // 2, (h % 2) * 64
                EBv = V(None, ebias_tr[h % 2])
                for i in range(NT):
                    j0 = min(max(i - 2, 0), NT - 5)
                    ty = 0 if 2 <= i <= 13 else (1 if i == 0 else 2 if i == 1 else 3 if i == 14 else 4)
                    sbk = (step % 3) * 2
                    eb_i = step % 3
                    obk = 6 + step % 2
                    if i == 8 and pend_exp is not None:
                        exp_ebias(pend_exp); pend_exp = None
                    for m in range(5):
                        bk_ = sbk + (0 if m < 4 else 1)
                        col = (m % 4) * 128
                        fw.op("pe", lambda e, m=m, bk_=bk_, col=col, i=i, j0=j0, hc=hc, hp=hp: e.matmul(
                            bank(bk_)[:, col:col + 128], lhsT=kT[hp:hp + 64, hc, (j0 + m) * 128:(j0 + m + 1) * 128],
                            rhs=qT[hp:hp + 64, hc, i * 128:(i + 1) * 128], start=True, stop=True),
                            [KT, QT], [V(None, b_tr[bk_])], inc=(m == 3 or m == 4))
                    if len(pendq) >= 2:
                        pendq.pop(0)()
                    fw.op("act", lambda e, sbk=sbk, eb_i=eb_i: e.activation(out=ebuf[eb_i][:, 0:512], in_=bank(sbk)[:, :], func=AF.Exp),
                          [V(None, b_tr[sbk])], [V(None, ebuf_tr[eb_i])])
                    fw.op("act", lambda e, sbk=sbk, eb_i=eb_i: e.activation(out=ebuf[eb_i][:, 512:640], in_=bank(sbk + 1)[:, 0:128], func=AF.Exp),
                          [V(None, b_tr[sbk + 1])], [V(None, ebuf_tr[eb_i])])
                    fw.op("dve", lambda e, eb_i=eb_i, ty=ty, h=h: e.tensor_tensor(
                        out=ptb[eb_i], in0=ebuf[eb_i], in1=ebias[h % 2][:, ty, :], op=ALU.mult),
                        [V(None, ebuf_tr[eb_i]), EBv], [V(None, ptb_tr[eb_i])])

                    def pv(i=i, j0=j0, h=h, eb_i=eb_i, obk=obk):
                        for m in range(5):
                            fw.op("pe", lambda e, m=m: e.matmul(
                                bank(obk)[:, 0:65], lhsT=ptb[eb_i][:, m * 128:(m + 1) * 128], rhs=v1[:, j0 + m, h, :],
                                start=(m == 0), stop=(m == 4)),
                                [V(None, ptb_tr[eb_i]), V1], [V(None, b_tr[obk])], inc=(m == 4))
                        fw.op("dve", lambda e: e.reciprocal(out=sm_sb[:, eb_i:eb_i + 1], in_=bank(obk)[:, 64:65]),
                              [V(None, b_tr[obk])], [V(None, sm_tr)])
                        fw.op("dve", lambda e: e.tensor_scalar(out=aout[:, i, h * 64:(h + 1) * 64], in0=bank(obk)[:, 0:64],
                                                                scalar1=sm_sb[:, eb_i:eb_i + 1], scalar2=None, op0=ALU.mult),
                              [V(None, b_tr[obk]), V(None, sm_tr)], [V(None, aout_tr[i])])
                    pendq.append(pv)
                    step += 1
                if h + 2 < 8:
                    while pendq:
                        pendq.pop(0)()
                    load_ebias(h + 2)
                    pend_exp = h + 2
            while pendq:
                pendq.pop(0)()

            ckpt(7)
            rms_stats(512, lambda i: aout[:, i, :], NT, lambda i: V(None, aout_tr[i]))
            for i in range(NT):
                fw.op("dve", lambda e, i=i: e.scalar_tensor_tensor(
                    out=anb, in0=aout[:, i, :], scalar=rstd_sb[:, i:i + 1], in1=aog_bc[:], op0=ALU.mult, op1=ALU.mult),
                    [V(None, aout_tr[i]), V(None, rstd_tr), V(None, aog_tr)], [ANB])
                bk = 6 + i % 2
                for c in range(4):
                    fw.op("pe", lambda e, c=c, bk=bk: e.transpose(
                        out=bankb(bk)[:, c * 128:(c + 1) * 128], in_=anb[:, c * 128:(c + 1) * 128], identity=ident_b[:]),
                        [ANB, CB], [V(None, b_tr[bk])], inc=(c == 3))
                fw.op("act", lambda e, i=i, bk=bk: e.copy(out=anT[:, :, i * 128:(i + 1) * 128],
                                                          in_=bankb(bk)[:, 0:512].rearrange("p (c s) -> p c s", s=128)),
                      [V(None, b_tr[bk])], [ANT])
            out_proj(s_wo1, anT, lambda c: ANT)

            ckpt(8)
            load_gain(n2g_d[l])
            RA.new_phase(); RS.new_phase(); RW.new_phase()
            h2, h2_tr = RA.carve([128, NT, D], BF16, ntr=NT)
            rms_stats(D, lambda i: x_sb[:, i, :], NT, lambda i: V(None, x_tr[i]))
            for stt in range(NT):
                fw.op("dve", lambda e, stt=stt: e.scalar_tensor_tensor(
                    out=h2[:, stt, :], in0=x_sb[:, stt, :], scalar=rstd_sb[:, stt:stt + 1], in1=g_bc[:],
                    op0=ALU.mult, op1=ALU.mult),
                    [V(None, x_tr[stt]), V(None, rstd_tr), V(None, g_tr)], [V(None, h2_tr[stt])])

            fw.dma("sp", lambda e: e.dma_start(out=h2_d.rearrange("(t p) d -> p t d", p=128), in_=h2),
                   [V(None, h2_tr)], [V(None, H2D)], "st_h2")
            gp, gp_tr, up_, up_tr, dpf, dp_tr = [None] * 3, [None] * 3, [None] * 3, [None] * 3, [None] * 3, [None] * 3
            a_, t_ = RW.carve([128, 4096], BF16); gp[0] = a_; gp_tr[0] = t_[0]
            a_, t_ = RW.carve([128, 4096], BF16); up_[0] = a_; up_tr[0] = t_[0]
            a_, t_ = RW.carve([128, 4096], BF16); dpf[0] = a_; dp_tr[0] = t_[0]
            a_, t_ = RW.carve([128, 4096], BF16); gp[1] = a_; gp_tr[1] = t_[0]
            a_, t_ = RS.carve([128, 4096], BF16); up_[1] = a_; up_tr[1] = t_[0]
            NSL = 3
            NPC = FF // 512
            pieces = [(e_, p_) for e_ in range(NE) for p_ in range(NPC)]

            def issue_gu(k, l=l):
                if k >= len(pieces):
                    return
                e_, p_ = pieces[k]
                si = k % NSL
                fsl = slice(p_ * 512, (p_ + 1) * 512)
                fw.dma("pool", lambda e: e.dma_start(
                    out=gp[si].rearrange("p (c f) -> p c f", f=512),
                    in_=wg_d[l * NE + e_][:, fsl].rearrange("(c p) f -> p c f", p=128)), [], [V(None, gp_tr[si])], f"ld_e{si}_g")
                fw.dma("pool", lambda e: e.dma_start(
                    out=up_[si].rearrange("p (c f) -> p c f", f=512),
                    in_=wu_d[l * NE + e_][:, fsl].rearrange("(c p) f -> p c f", p=128)), [], [V(None, up_tr[si])], f"ld_e{si}_u")

            def issue_d(k, l=l):
                if k >= len(pieces):
                    return
                e_, p_ = pieces[k]
                si = k % NSL
                fw.dma("pool", lambda e: e.dma_start(
                    out=dpf[si].rearrange("p (c f) -> p c f", f=1024),
                    in_=wd_d[l * NE + e_][p_ * 512:(p_ + 1) * 512, :].rearrange("(c p) f -> p c f", p=128)),
                    [], [V(None, dp_tr[si])], f"ld_e{si}_d")

            issue_gu(0)
            issue_d(0)
            issue_gu(1)

            ckpt(9)
            h2T, h2T_tr = [], []
            h2T_off = (RS.off + 63) // 64 * 64
            for i in range(2):
                a_, t_ = RS.carve([128, 8, 128], BF16); h2T.append(a_); h2T_tr.append(t_[0])
            assert RS.off == h2T_off + 4096
            aff, aff_tr = RS.carve([128, NT, NE], F32); AFF = V(None, aff_tr)
            rs_, rs_tr = RS.carve([128, NT], F32); RSV = V(None, rs_tr)
            Mf, Mf_tr = RS.carve([128, NT, NE], F32); MF = V(None, Mf_tr)
            Mb, Mb_tr = RS.carve([128, NT, NE], BF16); MB = V(None, Mb_tr)
            mr, mr_tr = RS.carve([128, NT, NE], F32); MR = V(None, mr_tr)
            Gt, Gt_tr = RS.carve([128, NT, NE], F32); GT = V(None, Gt_tr)
            sel, sel_tr = RS.carve([128, NT, CAP], BF16); SEL = V(None, sel_tr)
            selT, selT_tr = [], []
            for i in range(2):
                a_, t_ = RS.carve([128, 2, S], BF16); selT.append(a_); selT_tr.append(t_[0])
            xsT, xsT_tr = RS.carve([128, 8, CAP], BF16); XST = V(None, xsT_tr)
            hid, hid_tr = [], []
            for i in range(2):
                a_, t_ = RS.carve([128, CAP], BF16); hid.append(a_); hid_tr.append(t_[0])
            sg, sg_tr = [], []
            for i in range(2):
                a_, t_ = RS.carve([128, CAP], F32); sg.append(a_); sg_tr.append(t_[0])
            eo_sb, eo_tr = RS.carve([128, 2, D], BF16); EO = V(None, eo_tr)
            stmp, stmp_tr = [], []
            for i in range(2):
                a_, t_ = RS.carve([128, 512], F32); stmp.append(a_); stmp_tr.append(t_[0])
            bis, bis_tr = RS.carve([128, 8], F32); BIS = V(None, bis_tr)
            gebf, gebf_tr = RS.carve([128, 8], BF16); GEB = V(None, gebf_tr)
            affT = selT[0][0:16, :, :].rearrange("p a s -> p (a s)").bitcast(F32)
            AFT = V(None, selT_tr[0])
            bjunk = selT[1][0:16, 0, :]
            BJ = V(None, selT_tr[1])
            maskT = selT[1][0:16, :, :].rearrange("p a s -> p (a s)").bitcast(F32)

            LB = 0
            for stt in range(NT):
                bk = 1 + stt % 2
                hb = stt % 2
                for dc in range(8):
                    fw.op("pe", lambda e, stt=stt, dc=dc, bk=bk: e.transpose(
                        out=bankb(bk)[:, dc * 128:(dc + 1) * 128], in_=h2[:, stt, dc * 128:(dc + 1) * 128], identity=ident_b[:]),
                        [V(None, h2_tr[stt]), CB], [V(None, b_tr[bk])], inc=(dc == 7))
                copy(evac_eng(), h2T[hb], bankb(bk).rearrange("p (c s) -> p c s", s=128), [V(None, b_tr[bk])], [V(None, h2T_tr[hb])])
                for dc in range(8):
                    fw.op("pe", lambda e, stt=stt, dc=dc, hb=hb: e.matmul(
                        bank(LB)[:, stt * NE:(stt + 1) * NE], lhsT=h2T[hb][:, dc, :], rhs=wr_sb[:, dc, :],
                        start=(dc == 0), stop=(dc == 7)),
                        [V(None, h2T_tr[hb]), V(None, wr_tr)], [V(None, b_tr[LB])], inc=(dc == 7))
            fw.op("act", lambda e: e.activation(out=aff.rearrange("p t e -> p (t e)"), in_=bank(LB)[:, 0:NT * NE], func=AF.Exp),
                  [V(None, b_tr[LB])], [AFF])
            fw.op("dve", lambda e: e.tensor_reduce(out=rs_, in_=aff, axis=AX.X, op=ALU.add), [AFF], [RSV])
            fw.op("dve", lambda e: e.reciprocal(out=rs_, in_=rs_), [RSV], [RSV])
            fw.op("dve", lambda e: e.tensor_tensor(out=aff, in0=aff, in1=rs_.unsqueeze(2).to_broadcast([128, NT, NE]), op=ALU.mult),
                  [AFF, RSV], [AFF])
            for t in range(NT):
                bk = 4 + t // 4
                fw.op("pe", lambda e, t=t, bk=bk: e.transpose(
                    out=bank(bk)[0:16, (t % 4) * 128:(t % 4 + 1) * 128], in_=aff[:, t, :], identity=ident_f),
                    [AFF, CST], [V(None, b_tr[bk])], inc=(t % 4 == 3))
            for q4 in range(4):
                copy(evac_eng(), affT[:, q4 * 512:(q4 + 1) * 512], bank(4 + q4)[0:16, :], [V(None, b_tr[4 + q4])], [AFT])
            ckpt(10)
            affT8 = selT[0].rearrange("p a s -> p (a s)").bitcast(F32)
            junk8 = selT[1][:, 0, :]
            rep_tr = [Tr(deps_of([selT_tr[0]])) for _ in range(7)]
            for j_ in range(1, 8):
                fw.dma("sp", lambda e, j_=j_: e.dma_start(out=affT8[16 * j_:16 * j_ + 16, :], in_=affT8[0:16, :]),
                       [AFT], [V(None, rep_tr[j_ - 1])], f"ld_rep{j_}")
            REP = V(None, rep_tr)
            lo8, mid8, cnt8 = bis[:, 0:1], bis[:, 1:2], bis[:, 2:3]
            jvec = cst[:, 544:545]
            NPASS = 9
            fw.op("dve", lambda e: e.memset(bis[:, 0:1], 0.0), [], [BIS])
            fw.op("dve", lambda e: e.tensor_scalar(out=mid8, in0=jvec, scalar1=0.125, scalar2=None, op0=ALU.mult), [CST, BIS], [BIS])
            for ps in range(NPASS):
                stp = 8.0 ** -(ps + 1)
                fw.op("dve", lambda e: e.tensor_scalar(out=junk8, in0=affT8, scalar1=mid8, scalar2=None, op0=ALU.is_ge, op1=ALU.add,
                                                        accum_out=cnt8), [AFT, REP, BIS], [BJ, BIS])
                fw.op("dve", lambda e: e.tensor_single_scalar(out=gebf[:, 0:1], in_=cnt8, scalar=CAP - 0.5, op=ALU.is_ge), [BIS], [GEB])
                fw.op("pe", lambda e: e.matmul(bank(3)[:, 0:1], lhsT=modm_b[:], rhs=gebf[:, 0:1], start=True, stop=True),
                      [CB, GEB], [V(None, b_tr[3])])
                fw.op("dve", lambda e, stp=stp: e.scalar_tensor_tensor(out=lo8, in0=bank(3)[:, 0:1], scalar=stp, in1=lo8,
                                                                       op0=ALU.mult, op1=ALU.add), [V(None, b_tr[3]), BIS], [BIS])
                if ps + 1 < NPASS:
                    fw.op("dve", lambda e, stp=stp: e.scalar_tensor_tensor(out=mid8, in0=jvec, scalar=stp / 8.0, in1=lo8,
                                                                           op0=ALU.mult, op1=ALU.add), [CST, BIS], [BIS])
            lo = bis[0:16, 0:1]
            fw.op("dve", lambda e: e.tensor_scalar(out=maskT, in0=affT, scalar1=lo, scalar2=None, op0=ALU.is_ge),
                  [AFT, REP, BIS], [BJ, AFT])
            ckpt(11)
            for t in range(NT):
                fw.op("pe", lambda e, t=t: e.transpose(out=bank(1)[:, t * NE:(t + 1) * NE], in_=maskT[:, t * 128:(t + 1) * 128],
                                                       identity=cst[0:16, 0:16]),
                      [BJ, CST], [V(None, b_tr[1])], inc=(t == NT - 1))
            fw.op("dve", lambda e: e.tensor_copy(out=Mf.rearrange("p t e -> p (t e)"), in_=bank(1)[:, 0:NT * NE]), [V(None, b_tr[1])], [MF])
            fw.op("act", lambda e: e.copy(out=Mb.rearrange("p t e -> p (t e)"), in_=bank(1)[:, 0:NT * NE]), [V(None, b_tr[1])], [MB])
            for t in range(NT):
                for t2_ in range(t + 1):
                    fw.op("pe", lambda e, t=t, t2_=t2_: e.matmul(
                        bank(2)[:, t * NE:(t + 1) * NE], lhsT=(tri_b[:] if t2_ == t else ones_b[:]), rhs=Mb[:, t2_, :],
                        start=(t2_ == 0), stop=(t2_ == t)),
                        [CB, MB], [V(None, b_tr[2])], inc=(t2_ == t))
            fw.op("dve", lambda e: e.scalar_tensor_tensor(out=mr.rearrange("p t e -> p (t e)"), in0=bank(2)[:, 0:NT * NE], scalar=1.0,
                                                          in1=Mf.rearrange("p t e -> p (t e)"), op0=ALU.add, op1=ALU.mult),
                  [V(None, b_tr[2]), MF], [MR])
            fw.op("dve", lambda e: e.tensor_scalar(out=mr, in0=mr, scalar1=-1.0, scalar2=None, op0=ALU.add), [MR], [MR])
            fw.op("dve", lambda e: e.tensor_tensor(out=Gt, in0=aff, in1=Mf, op=ALU.mult), [AFF, MF], [GT])

            ckpt(12)
            pk = {"n": 0}

            def build_sel_tile(ex, t):
                fw.op("dve", lambda e: e.tensor_scalar(out=sel[:, t, :], in0=iota_c, scalar1=mr[:, t, ex:ex + 1],
                                                        scalar2=None, op0=ALU.is_equal),
                      [CST, MR], [SEL])

            _hd = deps_of(h2_tr)
            for j_, (lst, trl, si_) in enumerate(((dpf, dp_tr, 1), (gp, gp_tr, 2), (up_, up_tr, 2), (dpf, dp_tr, 2))):
                lst[si_] = RA.t[:, j_ * 4096:(j_ + 1) * 4096]
                trl[si_] = Tr(_hd)
                RA.live.append(trl[si_])
            issue_d(1)
            issue_gu(2)
            issue_d(2)
            _xd = deps_of(h2T_tr)
            xs_tm = RS.t[:, h2T_off // 2: h2T_off // 2 + 2048].rearrange("p (a d) -> p a d", d=D)
            xs_tr = [Tr(_xd), Tr(_xd)]
            RS.live += xs_tr

            def gather_a(ex):
                ib = ex % 2
                for ct in range(2):
                    for t in range(NT):
                        fw.op("pe", lambda e, t=t, ct=ct: e.matmul(
                            bank(6)[:, ct * 2:ct * 2 + 2], lhsT=sel[:, t, ct * 128:(ct + 1) * 128], rhs=tokc_b[:, t, :],
                            start=(t == 0), stop=(t == NT - 1)),
                            [SEL, CB], [V(None, b_tr[6])], inc=(t == NT - 1))
                fw.op("dve", lambda e: e.tensor_copy(out=idxf_sb[:, 0:4], in_=bank(6)[:, 0:4]), [V(None, b_tr[6])], [V(None, idxf_tr)])
                for ct in range(2):
                    fw.op("dve", lambda e, ct=ct: e.scalar_tensor_tensor(
                        out=idxf_sb[:, 4 + ct:5 + ct], in0=idxf_sb[:, ct * 2:ct * 2 + 1], scalar=128.0,
                        in1=idxf_sb[:, ct * 2 + 1:ct * 2 + 2], op0=ALU.mult, op1=ALU.add),
                        [V(None, idxf_tr)], [V(None, idxf_tr)])
                fw.op("dve", lambda e: e.tensor_copy(out=idx_sb[:, ib * 2:ib * 2 + 2], in_=idxf_sb[:, 4:6]),
                      [V(None, idxf_tr)], [V(None, idx_tr[ib])])
                for ct in range(2):
                    fw.dma("pool", lambda e, ct=ct: e.indirect_dma_start(
                        out=xs_tm[:, ct, :], out_offset=None, in_=h2_d[:, :],
                        in_offset=bass.IndirectOffsetOnAxis(ap=idx_sb[:, ib * 2 + ct:ib * 2 + ct + 1], axis=0)),
                        [V(None, idx_tr[ib]), V(None, H2D)], [V(None, xs_tr[ct])], f"ld_gx{ct}")

            def gather_b(ex):
                for ct in range(2):
                    bk = 6 + ct
                    for dc in range(8):
                        fw.op("pe", lambda e, ct=ct, dc=dc, bk=bk: e.transpose(
                            out=bankb(bk)[:, dc * 128:(dc + 1) * 128], in_=xs_tm[:, ct, dc * 128:(dc + 1) * 128], identity=ident_b[:]),
                            [V(None, xs_tr[ct]), CB], [V(None, b_tr[bk])], inc=(dc == 7))
                    copy(evac_eng(), xsT[:, :, ct * 128:(ct + 1) * 128], bankb(bk).rearrange("p (c s) -> p c s", s=128),
                         [V(None, b_tr[bk])], [XST])

            def make_selT(ex):
                sT = selT[ex % 2]; sT_tr = selT_tr[ex % 2]
                for ct in range(2):
                    for t in range(NT):
                        bk = 6 + t // 8
                        fw.op("pe", lambda e, t=t, ct=ct, bk=bk: e.transpose(
                            out=bankb(bk)[:, (t % 8) * 128:(t % 8 + 1) * 128], in_=sel[:, t, ct * 128:(ct + 1) * 128], identity=ident_b[:]),
                            [SEL, CB], [V(None, b_tr[bk])], inc=(t % 8 == 7))
                    for hh in range(2):
                        copy(evac_eng(), sT[:, ct, hh * 1024:(hh + 1) * 1024], bankb(6 + hh), [V(None, b_tr[6 + hh])], [V(None, sT_tr)])

            def eo_mm(fcg, si, hb, first, last):
                dbuf = dpf[si][:, (fcg // 2) * 2048:(fcg // 2 + 1) * 2048]
                dtr = dp_tr[si]
                for ct in range(2):
                    for dh in range(2):
                        bk = 2 + ct * 2 + dh
                        fw.op("pe", lambda e, ct=ct, dh=dh, bk=bk: e.matmul(
                            bank(bk)[:, :], lhsT=hid[hb][:, ct * 128:(ct + 1) * 128],
                            rhs=dbuf[:, (fcg % 2) * 1024 + dh * 512: (fcg % 2) * 1024 + (dh + 1) * 512],
                            start=first, stop=last),
                            [V(None, hid_tr[hb]), V(None, dtr)], [V(None, b_tr[bk])], inc=last or (ct == 1 and dh == 1))

            def ffn(ex, hook):
                pend_eo = None
                nchunk = 0
                for p_ in range(NPC):
                    kpiece = pk["n"]
                    si = kpiece % NSL
                    for fcg in range(4):
                        hb = nchunk % 2
                        for (wbuf, wtr, colo) in ((gp[si], gp_tr[si], 0), (up_[si], up_tr[si], 256)):
                            for dc in range(8):
                                fw.op("pe", lambda e, dc=dc, wbuf=wbuf, colo=colo, hb=hb, fcg=fcg: e.matmul(
                                    bank(hb)[:, colo:colo + 256],
                                    lhsT=wbuf[:, dc * 512 + fcg * 128: dc * 512 + (fcg + 1) * 128],
                                    rhs=xsT[:, dc, :], start=(dc == 0), stop=(dc == 7)),
                                    [V(None, wtr), XST], [V(None, b_tr[hb])], inc=(dc == 7))
                        if fcg == 3:
                            issue_gu(kpiece + NSL)
                        if pend_eo is not None:
                            pend_eo()
                        fw.op("act", lambda e, hb=hb: e.activation(out=sg[hb], in_=bank(hb)[:, 0:256], func=AF.Silu),
                              [V(None, b_tr[hb])], [V(None, sg_tr[hb])])
                        fw.op("dve", lambda e, hb=hb: e.tensor_tensor(out=hid[hb], in0=bank(hb)[:, 256:512], in1=sg[hb], op=ALU.mult),
                              [V(None, b_tr[hb]), V(None, sg_tr[hb])], [V(None, hid_tr[hb])])
                        hook(nchunk)
                        first = (nchunk == 0)
                        last = (nchunk == 4 * NPC - 1)

                        def mk(fcg=fcg, si=si, hb=hb, first=first, last=last, rel=(fcg == 3), kpiece=kpiece):
                            def f():
                                eo_mm(fcg, si, hb, first, last)
                                if rel:
                                    issue_d(kpiece + NSL)
                            return f
                        pend_eo = mk()
                        nchunk += 1
                    pk["n"] += 1
                pend_eo()

            def eo_evac():
                for ct in range(2):
                    for dh in range(2):
                        bk = 2 + ct * 2 + dh
                        copy(evac_eng(), eo_sb[:, ct, dh * 512:(dh + 1) * 512], bank(bk)[:, :], [V(None, b_tr[bk])], [EO])

            def scatter(ex):
                sT = selT[ex % 2]; sT_tr = selT_tr[ex % 2]
                for stt in range(NT):
                    for dh in range(2):
                        bk = (6, 7, 0, 1, 2, 3, 4, 5)[(stt * 2 + dh) % 8]
                        for ct in range(2):
                            fw.op("pe", lambda e, stt=stt, dh=dh, ct=ct, bk=bk: e.matmul(
                                bank(bk)[:, :], lhsT=sT[:, ct, stt * 128:(stt + 1) * 128], rhs=eo_sb[:, ct, dh * 512:(dh + 1) * 512],
                                start=(ct == 0), stop=(ct == 1)),
                                [V(None, sT_tr), EO], [V(None, b_tr[bk])], inc=(ct == 1))
                        XH = V(None, x_tr[stt][dh])
                        if dh == 0:
                            fw.op("dve", lambda e, stt=stt, dh=dh, bk=bk: e.scalar_tensor_tensor(
                                out=x_sb[:, stt, dh * 512:(dh + 1) * 512], in0=bank(bk)[:, :], scalar=Gt[:, stt, ex:ex + 1],
                                in1=x_sb[:, stt, dh * 512:(dh + 1) * 512], op0=ALU.mult, op1=ALU.add),
                                [V(None, b_tr[bk]), GT, XH], [XH])
                        else:
                            k_ = stt % 2
                            fw.op("act", lambda e, stt=stt, bk=bk, k_=k_: e.activation(
                                out=stmp[k_], in_=bank(bk)[:, :], func=AF.Copy, scale=Gt[:, stt, ex:ex + 1]),
                                [V(None, b_tr[bk]), GT], [V(None, stmp_tr[k_])])
                            fw.op("pool", lambda e, stt=stt, dh=dh, k_=k_: e.tensor_tensor(
                                out=x_sb[:, stt, dh * 512:(dh + 1) * 512], in0=x_sb[:, stt, dh * 512:(dh + 1) * 512],
                                in1=stmp[k_], op=ALU.add),
                                [V(None, stmp_tr[k_]), XH], [XH])

            for t in range(NT):
                build_sel_tile(0, t)
            gather_a(0)
            make_selT(0)
            gather_b(0)
            def mk_hook(nxt):
                def hook(n):
                    if nxt >= NE:
                        return
                    if n < 8:
                        build_sel_tile(nxt, 2 * n)
                        build_sel_tile(nxt, 2 * n + 1)
                    elif n == 8:
                        gather_a(nxt)
                return hook

            for ex in range(NE):
                nxt = ex + 1
                ffn(ex, mk_hook(nxt))
                eo_evac()
                if nxt < NE:
                    make_selT(nxt)
                scatter(ex)
                if nxt < NE:
                    gather_b(nxt)

          except _Stop:
            pass
        load_gain(fg_d[0])
        rms_stats(D, lambda i: x_sb[:, i, :], NT, lambda i: V(None, x_tr[i]))
        for stt in range(NT):
            fw.op("dve", lambda e, stt=stt: e.scalar_tensor_tensor(
                out=x_sb[:, stt, :], in0=x_sb[:, stt, :], scalar=rstd_sb[:, stt:stt + 1], in1=g_bc[:], op0=ALU.mult, op1=ALU.mult),
                [V(None, x_tr[stt]), V(None, rstd_tr), V(None, g_tr)], [V(None, x_tr[stt])])
        for t4 in range(4):
            fw.dma("sp", lambda e, t4=t4: e.dma_start(
                out=out_d[t4 * 512:(t4 + 1) * 512, :].rearrange("(t p) d -> p t d", p=128),
                in_=x_sb[:, t4 * 4:(t4 + 1) * 4, :]),
                [V(None, xflat(t4 * 4, (t4 + 1) * 4))], [], f"st_o{t4}")
        fw.final_wait("sp", [f"st_o{t4}" for t4 in range(4)])
        fw.emit()
    return nc


def _ebias_table(rpb):
    L = rpb.shape[0]
    p = np.arange(128)
    ka, kc = p // 64, p % 64
    qa, qc = p // 64, p % 64
    out = np.empty((L, 8, 128, 5, 5, 128), np.float32)
    win_start = np.clip(qc - 8, 0, 48)
    colmask = (kc[:, None] >= win_start[None, :]) & (kc[:, None] < win_start[None, :] + 16)
    coloff = np.clip(kc[:, None] - qc[None, :] + 15, 0, 30)
    for ty, i in enumerate((2, 0, 1, 14, 15)):
        j0 = min(max(i - 2, 0), 11)
        for m in range(5):
            kr = 2 * (j0 + m) + ka
            qr = 2 * i + qa
            rs = np.clip(qr - 4, 0, 24)
            rowmask = (kr[:, None] >= rs[None, :]) & (kr[:, None] < rs[None, :] + 8)
            rowoff = np.clip(kr[:, None] - qr[None, :] + 7, 0, 14)
            vals = rpb[:, :, rowoff, coloff]
            out[:, :, :, ty, m, :] = np.where((rowmask & colmask)[None, None], vals, np.float32(NEG))
    return np.ascontiguousarray(out.reshape(L * 8, 128, 3200))


def _consts():
    c = np.zeros((128, 704), np.float32)
    c[:, 0:128] = np.eye(128, dtype=np.float32)
    k = np.arange(128)
    c[:, 128:256] = (k[:, None] < k[None, :]).astype(np.float32)
    c[:, 256:512] = np.arange(256, dtype=np.float32)[None, :]
    c[:, 512:544:2] = np.arange(16, dtype=np.float32)[None, :]
    c[:, 513:544:2] = k[:, None].astype(np.float32)
    c[:, 544] = (k // 16 + 1).astype(np.float32)
    c[:, 576:704] = ((k[:, None] % 16) == (k[None, :] % 16)).astype(np.float32)
    return c


def prep(inputs, nl=DEPTH):
    f = lambda a: np.ascontiguousarray(np.asarray(a, dtype=np.float32))
    conv_w = f(inputs["conv_w"])
    cw = conv_w.reshape(DEPTH, 31, 4, 128).transpose(0, 3, 2, 1).reshape(DEPTH, 128, 124)
    vecs = np.stack([f(inputs["conv_b"]), f(inputs["conv_ln_g"]), f(inputs["conv_ln_b"]), f(inputs["conv_out_g"])], 1)
    cvec = vecs.reshape(DEPTH, 4, 4, 128).transpose(0, 3, 1, 2).reshape(DEPTH, 128, 16)
    shared = {
        "norm1_g": f(inputs["norm1_g"]), "norm2_g": f(inputs["norm2_g"]), "final_g": f(inputs["final_g"]).reshape(1, D),
        "attn_out_g": f(inputs["attn_out_g"]), "w_in": f(inputs["w_in"]), "w_out": f(inputs["w_out"]),
        "w_router": f(inputs["w_router"]),
        "w_gate": f(inputs["w_gate"]).reshape(DEPTH * NE, D, FF), "w_up": f(inputs["w_up"]).reshape(DEPTH * NE, D, FF),
        "w_down": f(inputs["w_down"]).reshape(DEPTH * NE, FF, D),
        "cw": np.ascontiguousarray(cw), "cvec": np.ascontiguousarray(cvec),
        "ebias": _ebias_table(f(inputs["rpb"])), "consts": _consts(),
    }
    return shared


_NC_CACHE = {}


def kernel(**inputs):
    x = np.ascontiguousarray(np.asarray(inputs["x"], dtype=np.float32))
    B = x.shape[0]
    shared = prep(inputs)
    if DEPTH not in _NC_CACHE:
        _NC_CACHE[DEPTH] = build(DEPTH)
    nc = _NC_CACHE[DEPTH]
    in_maps = [dict(shared, x=x[b]) for b in range(B)]
    res = run_bass_kernel_spmd(nc, in_maps, core_ids=list(range(B)))
    return np.stack([res.results[b]["out"] for b in range(B)], 0).astype(np.float32)
```

```python
import contextlib
import os
import numpy as np
import concourse.bass as bass
import concourse.mybir as mybir
from concourse.bass_utils import run_bass_kernel_spmd

F32 = mybir.dt.float32
BF16 = mybir.dt.bfloat16
AF = mybir.ActivationFunctionType
ALU = mybir.AluOpType
AX = mybir.AxisListType

D = 1024
S = 2048
DEPTH = 4
NT = 16
NE = 16
CAP = 256
FF = 2048
EPS = 1e-6
NEG = -30000.0
NBIS = 26
STOP = int(os.environ.get("KSTOP", "99"))


class _Stop(Exception):
    pass


def ckpt(k):
    if STOP == k:
        raise _Stop()


class Tr:
    __slots__ = ("w", "r", "psum")

    def __init__(self, deps=None, psum=False):
        self.w = None
        self.r = dict(deps) if deps else {}
        self.psum = psum


class V:
    __slots__ = ("ap", "trs")

    def __init__(self, ap, trs):
        self.ap = ap
        self.trs = trs if isinstance(trs, (list, tuple)) else [trs]


class Eng:
    def __init__(self, key):
        self.key = key
        self.cnt = 0
        self.seen = {}
        self.q = []


class FW:
    def __init__(self, nc, stack):
        self.nc = nc
        self.stack = stack
        self.sems = {}
        self.eng = {}
        for k in ("pe", "act", "dve", "pool", "sp"):
            self.eng[k] = Eng(k)
            self.sems[k] = stack.enter_context(nc.semaphore("s_" + k))
        self.dval = {}

    def dsem(self, key):
        if key not in self.sems:
            self.sems[key] = self.stack.enter_context(self.nc.semaphore("d_" + key))
            self.dval[key] = 0
        return key

    def _waits(self, E, needs):
        for k, val in needs.items():
            if val <= 0 or E.seen.get(k, 0) >= val:
                continue
            if k in self.eng and val > self.eng[k].cnt:
                raise RuntimeError(f"dangling wait: {E.key} needs {k}>={val} (cnt={self.eng[k].cnt})")
            E.seen[k] = val
            E.q.append(("wait", k, val))

    def _needs(self, E, reads, writes):
        needs = {}

        def need(k, v, war=False):
            if k == E.key and (E.key == "pe" or war):
                return
            if v > needs.get(k, 0):
                needs[k] = v

        for v_ in reads:
            for t in v_.trs:
                if t.w:
                    need(*t.w)
                if t.psum:
                    for k, val in t.r.items():
                        need(k, val, war=True)
        for v_ in writes:
            for t in v_.trs:
                if t.w:
                    need(*t.w)
                for k, val in t.r.items():
                    need(k, val, war=True)
        return needs

    def op(self, en, fn, reads=(), writes=(), inc=True):
        E = self.eng[en]
        self._waits(E, self._needs(E, reads, writes))
        if inc:
            E.cnt += 1
            mark = E.cnt
        else:
            mark = E.cnt + 1
        E.q.append(("op", fn, inc))
        for v_ in reads:
            for t in v_.trs:
                if t.r.get(E.key, 0) < mark:
                    t.r[E.key] = mark
        for v_ in writes:
            for t in v_.trs:
                t.w = (E.key, mark)
                t.r = {}

    def dma(self, en, fn, reads, writes, dsem):
        E = self.eng[en]
        self.dsem(dsem)
        needs = self._needs(E, reads, writes)
        if self.dval[dsem] > needs.get(dsem, 0):
            needs[dsem] = self.dval[dsem]
        self._waits(E, needs)
        self.dval[dsem] += 16
        val = self.dval[dsem]
        E.q.append(("dma", fn, dsem))
        for v_ in reads:
            for t in v_.trs:
                t.r[dsem] = val
        for v_ in writes:
            for t in v_.trs:
                t.w = (dsem, val)
                t.r = {}

    def final_wait(self, en, keys):
        self._waits(self.eng[en], {k: self.dval[k] for k in keys})

    def emit(self):
        for E in self.eng.values():
            assert E.cnt < 60000, (E.key, E.cnt)
        for k, v in self.dval.items():
            assert v < 60000, (k, v)
        with self.nc.Block() as block:
            def run(E):
                def body(e):
                    for it in E.q:
                        if it[0] == "wait":
                            e.wait_ge(self.sems[it[1]], it[2])
                        elif it[0] == "op":
                            ins = it[1](e)
                            if it[2]:
                                ins.then_inc(self.sems[E.key], 1)
                        else:
                            it[1](e).then_inc(self.sems[it[2]], 16)
                return body
            block.sync(run(self.eng["sp"]))
            block.tensor(run(self.eng["pe"]))
            block.scalar(run(self.eng["act"]))
            block.vector(run(self.eng["dve"]))
            block.gpsimd(run(self.eng["pool"]))


def deps_of(trs):
    d = {}
    for t in trs:
        if t.w and t.w[1] > d.get(t.w[0], 0):
            d[t.w[0]] = t.w[1]
        for k, v in t.r.items():
            if v > d.get(k, 0):
                d[k] = v
    return d


class Region:
    def __init__(self, nc, stack, name, nbytes):
        self.t = stack.enter_context(nc.sbuf_tensor(name, [128, nbytes // 2], BF16))
        self.nbytes = nbytes
        self.live = []
        self.base = {}
        self.off = 0

    def new_phase(self):
        self.base = deps_of(self.live + [Tr(self.base)])
        self.live = []
        self.off = 0

    def carve(self, shape, dtype, ntr=1, at=None):
        es = 4 if dtype == F32 else 2
        n = int(np.prod(shape[1:]))
        nb = n * es
        off = self.off if at is None else at
        off = (off + 63) // 64 * 64
        assert off + nb <= self.nbytes, (self.t, off, nb, self.nbytes)
        if at is None:
            self.off = off + nb
        ap = self.t[0:shape[0], off // 2: (off + nb) // 2]
        if dtype != BF16:
            ap = ap.bitcast(dtype)
        if len(shape) == 3:
            ap = ap.rearrange("p (a b) -> p a b", b=shape[2])
        elif len(shape) == 4:
            ap = ap.rearrange("p (a b c) -> p a b c", b=shape[2], c=shape[3])
        trs = [Tr(self.base) for _ in range(ntr)]
        self.live += trs
        return ap, trs


def build(nl=DEPTH):
    nc = bass.Bass("TRN2", target_bir_lowering=False)

    def din(name, shape, dt=F32):
        return nc.dram_tensor(name, shape, dt, kind="ExternalInput").ap()

    x_d = din("x", [S, D])
    n1g_d = din("norm1_g", [DEPTH, D])
    n2g_d = din("norm2_g", [DEPTH, D])
    fg_d = din("final_g", [1, D])
    aog_d = din("attn_out_g", [DEPTH, 512])
    win_d = din("w_in", [DEPTH, D, 2560])
    wout_d = din("w_out", [DEPTH, D, D])
    wr_d = din("w_router", [DEPTH, D, NE])
    wg_d = din("w_gate", [DEPTH * NE, D, FF])
    wu_d = din("w_up", [DEPTH * NE, D, FF])
    wd_d = din("w_down", [DEPTH * NE, FF, D])
    cw_d = din("cw", [DEPTH, 128, 124])
    cvec_d = din("cvec", [DEPTH, 128, 16])
    eb_d = din("ebias", [DEPTH * 8, 128, 3200])
    cst_d = din("consts", [128, 704])
    out_d = nc.dram_tensor("out", [S, D], F32, kind="ExternalOutput").ap()
    h2_d = nc.dram_tensor("h2_scr", [S, D], BF16, kind="Internal").ap()

    with contextlib.ExitStack() as st:
        fw = FW(nc, st)
        sb = lambda n, s, d: st.enter_context(nc.sbuf_tensor(n, s, d))
        NOTR = []

        x_sb = sb("x_sb", [128, NT, D], F32)
        x_tr = [[Tr(), Tr()] for _ in range(NT)]
        xflat = lambda a, b: [t_ for pair in x_tr[a:b] for t_ in pair]
        RA = Region(nc, st, "RA", 32768)
        RS = Region(nc, st, "RS", 61440)
        RW = Region(nc, st, "RW", 36864)
        cst = sb("cst", [128, 704], F32); cst_tr = Tr()
        ident_f = cst[:, 0:128]
        iota_c = cst[:, 256:512]
        ident_b = sb("ident_b", [128, 128], BF16)
        tri_b = sb("tri_b", [128, 128], BF16)
        ones_b = sb("ones_b", [128, 128], BF16)
        onesd_b = sb("onesd_b", [128, 128], BF16)
        cb_tr = Tr()
        tokc_b = sb("tokc_b", [128, NT, 2], BF16)
        modm_b = sb("modm_b", [128, 128], BF16)
        idx_sb = sb("idx_sb", [128, 4], mybir.dt.int32); idx_tr = [Tr(), Tr()]
        H2D = Tr()
        idxf_sb = sb("idxf_sb", [128, 8], F32); idxf_tr = Tr()
        g_bc = sb("g_bc", [128, D], F32); g_tr = Tr()
        aog_bc = sb("aog_bc", [128, 512], F32); aog_tr = Tr()
        cw_sb = sb("cw_sb", [128, 124], F32); cw_tr = Tr()
        cvec_sb = sb("cvec_sb", [128, 16], F32); cvec_tr = Tr()
        wr_sb = sb("wr_sb", [128, 8, NE], BF16); wr_tr = Tr()
        ss_sb = sb("ss_sb", [128, NT], F32); ss_tr = [Tr() for _ in range(4)]
        rstd_sb = sb("rstd_sb", [128, NT], F32); rstd_tr = [Tr() for _ in range(4)]
        sm_sb = sb("sm_sb", [128, 64], F32); sm_tr = Tr()
        junk_sb = sb("junk_sb", [128, D], BF16); junk_tr = Tr()

        banks = [st.enter_context(nc.psum_tensor(f"bank{i}", [128, 512], F32)) for i in range(8)]
        b_tr = [Tr(psum=True) for _ in range(8)]

        def bank(i):
            return banks[i]

        def bankb(i):
            return banks[i][:].bitcast(BF16)

        rr = {"ev": 0}

        def evac_eng():
            rr["ev"] ^= 1
            return "act" if rr["ev"] else "dve"

        def copy(en, out, in_, reads, writes):
            if en == "act":
                fw.op("act", lambda e: e.copy(out=out, in_=in_), reads, writes)
            else:
                fw.op(en, lambda e: e.tensor_copy(out=out, in_=in_), reads, writes)

        fw.dma("sp", lambda e: e.dma_start(out=cst[:], in_=cst_d[:, :]), [], [V(None, cst_tr)], "ld_c")
        for t4 in range(4):
            fw.dma("sp", lambda e, t4=t4: e.dma_start(
                out=x_sb[:, t4 * 4:(t4 + 1) * 4, :],
                in_=x_d[t4 * 512:(t4 + 1) * 512, :].rearrange("(t p) d -> p t d", p=128)),
                [], [V(None, xflat(t4 * 4, (t4 + 1) * 4))], f"ld_x{t4}")
        fw.op("dve", lambda e: e.tensor_copy(out=ident_b[:], in_=cst[:, 0:128]), [V(None, cst_tr)], [V(None, cb_tr)])
        fw.op("dve", lambda e: e.tensor_copy(out=tri_b[:], in_=cst[:, 128:256]), [V(None, cst_tr)], [V(None, cb_tr)])
        fw.op("dve", lambda e: e.memset(ones_b[:], 1.0), [], [V(None, cb_tr)])
        fw.op("dve", lambda e: e.tensor_copy(out=modm_b[:], in_=cst[:, 576:704]), [V(None, cst_tr)], [V(None, cb_tr)])
        fw.op("dve", lambda e: e.tensor_copy(out=tokc_b[:].rearrange("p t k -> p (t k)"), in_=cst[:, 512:544]), [V(None, cst_tr)], [V(None, cb_tr)])
        fw.op("dve", lambda e: e.memset(onesd_b[:], 1.0 / 512.0), [], [V(None, cb_tr)])
        CB = V(None, cb_tr)
        CST = V(None, cst_tr)

        def load_gain(src_row_ap):
            fw.dma("sp", lambda e: e.dma_start(out=g_bc[:], in_=src_row_ap.partition_broadcast(128)),
                   [], [V(None, g_tr)], "ld_g")

        def norm_loop(width, get_in, in_reads):
            def stats(g):
                lo_, hi_ = 4 * g, 4 * g + 4
                for i in range(lo_, hi_):
                    fw.op("act", lambda e, i=i: e.activation(out=junk_sb[:, 0:width], in_=get_in(i), func=AF.Square,
                                                            accum_out=ss_sb[:, i:i + 1]),
                          [in_reads(i)], [V(None, junk_tr), V(None, ss_tr[g])])
                fw.op("act", lambda e: e.activation(out=rstd_sb[:, lo_:hi_], in_=ss_sb[:, lo_:hi_], func=AF.Ln,
                                                    bias=EPS, scale=1.0 / width),
                      [V(None, ss_tr[g])], [V(None, rstd_tr[g])])
                fw.op("act", lambda e: e.activation(out=rstd_sb[:, lo_:hi_], in_=rstd_sb[:, lo_:hi_], func=AF.Exp, scale=-0.5),
                      [V(None, rstd_tr[g])], [V(None, rstd_tr[g])])
            stats(0)
            for g in range(4):
                if g + 1 < 4:
                    stats(g + 1)
                for i in range(4 * g, 4 * g + 4):
                    yield i, V(None, rstd_tr[g])

        for l in range(nl if STOP == 99 else 1):
          try:
            load_gain(n1g_d[l])
            fw.dma("sp", lambda e, l=l: e.dma_start(out=aog_bc[:], in_=aog_d[l].partition_broadcast(128)), [], [V(None, aog_tr)], "ld_p")
            fw.dma("sp", lambda e, l=l: e.dma_start(out=cw_sb[:], in_=cw_d[l]), [], [V(None, cw_tr)], "ld_p")
            fw.dma("sp", lambda e, l=l: e.dma_start(out=cvec_sb[:], in_=cvec_d[l]), [], [V(None, cvec_tr)], "ld_p")
            fw.dma("pool", lambda e, l=l: e.dma_start(out=wr_sb[:], in_=wr_d[l].rearrange("(c p) e -> p c e", p=128)),
                   [], [V(None, wr_tr)], "ld_wr")

            RA.new_phase(); RS.new_phase(); RW.new_phase()
            hT, hT_tr = RA.carve([128, 8, S], BF16, ntr=NT)
            HT = V(None, hT_tr)
            htile, htile_tr = [], []
            for i in range(2):
                a_, t_ = RS.carve([128, D], BF16)
                htile.append(a_); htile_tr.append(t_[0])
            wslot, wslot_tr = [], []
            for i in range(3):
                a_, t_ = RW.carve([128, 6144], BF16)
                wslot.append(a_); wslot_tr.append(t_[0])
            wq = {"n": 0}

            def wload(fns):
                si = wq["n"] % 3
                wq["n"] += 1
                for f in fns:
                    o, i_ = f(wslot[si])
                    fw.dma("pool", lambda e, o=o, i_=i_: e.dma_start(out=o, in_=i_), [], [V(None, wslot_tr[si])], f"ld_w{si}")
                return si

            def win_piece(cb, l=l):
                return [lambda sl: (sl[:, 0:4096].rearrange("p (c f) -> p c f", f=512),
                                    win_d[l][:, cb * 512:(cb + 1) * 512].rearrange("(c p) f -> p c f", p=128))]

            def wout_piece(half, l=l):
                return [lambda sl: (sl[:, 0:4096].rearrange("p (c f) -> p c f", f=1024),
                                    wout_d[l][half * 512:(half + 1) * 512, :].rearrange("(c p) f -> p c f", p=128))]

            s_g = wload(win_piece(4))
            s_a = wload(win_piece(3))

            for stt, RSTD in norm_loop(D, lambda i: x_sb[:, i, :], lambda i: V(None, x_tr[i])):
                hb = stt % 2
                fw.op("dve", lambda e, stt=stt, hb=hb: e.scalar_tensor_tensor(
                    out=htile[hb], in0=x_sb[:, stt, :], scalar=rstd_sb[:, stt:stt + 1], in1=g_bc[:],
                    op0=ALU.mult, op1=ALU.mult),
                    [V(None, x_tr[stt]), RSTD, V(None, g_tr)], [V(None, htile_tr[hb])])
                bk = stt % 2
                for dc in range(8):
                    fw.op("pe", lambda e, dc=dc, hb=hb, bk=bk: e.transpose(
                        out=bankb(bk)[:, dc * 128:(dc + 1) * 128], in_=htile[hb][:, dc * 128:(dc + 1) * 128], identity=ident_b[:]),
                        [V(None, htile_tr[hb]), CB], [V(None, b_tr[bk])], inc=(dc == 7))
                copy(evac_eng(), hT[:, :, stt * 128:(stt + 1) * 128], bankb(bk).rearrange("p (c s) -> p c s", s=128),
                     [V(None, b_tr[bk])], [V(None, hT_tr[stt])])

            ckpt(1)
            RS.new_phase()
            upad, upad_tr = RS.carve([128, 4, S + 30], BF16)
            UP = V(None, upad_tr)
            diag, diag_tr = RS.carve([128, 31, 128], BF16)
            DG = V(None, diag_tr)
            cvn, cvn_tr = RS.carve([128, 4, S], BF16, ntr=4)
            zbuf, z_tr = RS.carve([128, 4, 512], F32, ntr=4)
            t1, t1_tr = RS.carve([128, 512], F32)
            t2, t2_tr = RS.carve([128, 512], F32)
            sqb, sqb_tr = RS.carve([128, 512], BF16)
            T1 = V(None, t1_tr); T2 = V(None, t2_tr); SQB = V(None, sqb_tr)
            fw.op("dve", lambda e: e.memset(upad[:, :, 0:15], 0.0), [], [UP])
            fw.op("dve", lambda e: e.memset(upad[:, :, S + 15:S + 30], 0.0), [], [UP])

            bkc = {"n": 0}

            def nbank(lo=0, hi=8):
                b = lo + bkc["n"] % (hi - lo)
                bkc["n"] += 1
                return b

            def proj_fm(slot, evac):
                for fc in range(4):
                    for sblk in range(4):
                        bk = nbank()
                        for dc in range(8):
                            fw.op("pe", lambda e, fc=fc, sblk=sblk, dc=dc, bk=bk, slot=slot: e.matmul(
                                bank(bk)[:, :], lhsT=wslot[slot][:, dc * 512 + fc * 128: dc * 512 + (fc + 1) * 128],
                                rhs=hT[:, dc, sblk * 512:(sblk + 1) * 512], start=(dc == 0), stop=(dc == 7)),
                                [V(None, wslot_tr[slot]), HT], [V(None, b_tr[bk])], inc=(dc == 7))
                        evac(fc, sblk, bk)

            proj_fm(s_g, lambda fc, sblk, bk: fw.op("act", lambda e: e.activation(
                out=upad[:, fc, 15 + sblk * 512: 15 + (sblk + 1) * 512], in_=bank(bk)[:, :], func=AF.Sigmoid),
                [V(None, b_tr[bk])], [UP]))
            s_wo2 = wload(wout_piece(1))
            proj_fm(s_a, lambda fc, sblk, bk: fw.op("dve", lambda e: e.tensor_tensor(
                out=upad[:, fc, 15 + sblk * 512: 15 + (sblk + 1) * 512], in0=bank(bk)[:, :],
                in1=upad[:, fc, 15 + sblk * 512: 15 + (sblk + 1) * 512], op=ALU.mult),
                [V(None, b_tr[bk]), UP], [UP]))
            s_q = wload(win_piece(0))
            s_k = wload(win_piece(1))

            ckpt(2)
            for c in range(4):
                fw.op("dve", lambda e, c=c: e.tensor_tensor(
                    out=diag, in0=ident_b[:].unsqueeze(1).to_broadcast([128, 31, 128]),
                    in1=cw_sb[:, c * 31:(c + 1) * 31].unsqueeze(2).to_broadcast([128, 31, 128]), op=ALU.mult),
                    [CB, V(None, cw_tr)], [DG])
                for sblk in range(4):
                    bk = nbank()
                    for j in range(31):
                        fw.op("pe", lambda e, c=c, j=j, sblk=sblk, bk=bk: e.matmul(
                            bank(bk)[:, :], lhsT=diag[:, j, :], rhs=upad[:, c, sblk * 512 + j: sblk * 512 + j + 512],
                            start=(j == 0), stop=(j == 30)),
                            [DG, UP], [V(None, b_tr[bk])], inc=(j == 30))
                    fw.op("act", lambda e, c=c, sblk=sblk, bk=bk: e.activation(
                        out=cvn[:, c, sblk * 512:(sblk + 1) * 512], in_=bank(bk)[:, :], func=AF.Identity,
                        bias=cvec_sb[:, c:c + 1]),
                        [V(None, b_tr[bk]), V(None, cvec_tr)], [V(None, cvn_tr[c])])

            def out_proj(slot, srcT, src_reads, tiles=range(NT)):
                for stt in tiles:
                    for dh in range(2):
                        bk = nbank()
                        for c in range(4):
                            fw.op("pe", lambda e, stt=stt, dh=dh, c=c, bk=bk: e.matmul(
                                bank(bk)[:, :], lhsT=srcT[:, c, stt * 128:(stt + 1) * 128],
                                rhs=wslot[slot][:, c * 1024 + dh * 512: c * 1024 + (dh + 1) * 512],
                                start=(c == 0), stop=(c == 3)),
                                [src_reads(c), V(None, wslot_tr[slot])], [V(None, b_tr[bk])], inc=(c == 3))
                        fw.op("dve", lambda e, stt=stt, dh=dh, bk=bk: e.tensor_tensor(
                            out=x_sb[:, stt, dh * 512:(dh + 1) * 512], in0=bank(bk)[:, :],
                            in1=x_sb[:, stt, dh * 512:(dh + 1) * 512], op=ALU.add),
                            [V(None, b_tr[bk]), V(None, x_tr[stt][dh])], [V(None, x_tr[stt][dh])])

            ckpt(3)
            for sblk in range(4):
                sl = slice(sblk * 512, (sblk + 1) * 512)
                bm, be, b3 = nbank(), nbank(), nbank()
                for c in range(4):
                    fw.op("pe", lambda e, c=c, bm=bm, sl=sl: e.matmul(bank(bm)[:, :], lhsT=onesd_b[:], rhs=cvn[:, c, sl],
                                                                start=(c == 0), stop=(c == 3)),
                          [CB, V(None, cvn_tr[c])], [V(None, b_tr[bm])], inc=(c == 3))
                for c in range(4):
                    fw.op("act", lambda e, c=c, sl=sl: e.activation(out=sqb, in_=cvn[:, c, sl], func=AF.Square),
                          [V(None, cvn_tr[c])], [SQB])
                    fw.op("pe", lambda e, c=c, be=be: e.matmul(bank(be)[:, :], lhsT=onesd_b[:], rhs=sqb,
                                                                start=(c == 0), stop=(c == 3)),
                          [CB, SQB], [V(None, b_tr[be])], inc=True)
                fw.op("dve", lambda e, bm=bm: e.tensor_tensor(out=t1, in0=bank(bm)[:, :], in1=bank(bm)[:, :], op=ALU.mult)
                      if False else e.tensor_copy(out=t1, in_=bank(bm)[:, :]), [V(None, b_tr[bm])], [T1])
                fw.op("dve", lambda e: e.tensor_tensor(out=t2, in0=t1, in1=t1, op=ALU.mult), [T1], [T2])
                fw.op("dve", lambda e, be=be: e.tensor_tensor(out=t2, in0=bank(be)[:, :], in1=t2, op=ALU.subtract),
                      [V(None, b_tr[be]), T2], [T2])
                fw.op("act", lambda e: e.activation(out=t2, in_=t2, func=AF.Ln, bias=EPS), [T2], [T2])
                fw.op("act", lambda e: e.activation(out=t2, in_=t2, func=AF.Exp, scale=-0.5), [T2], [T2])
                for c in range(4):
                    ZC = V(None, z_tr[c])
                    fw.op("dve", lambda e, c=c, sl=sl: e.tensor_tensor(out=zbuf[:, c, :], in0=cvn[:, c, sl], in1=t1, op=ALU.subtract),
                          [V(None, cvn_tr[c]), T1], [ZC])
                    fw.op("dve", lambda e, c=c: e.tensor_tensor(out=zbuf[:, c, :], in0=zbuf[:, c, :], in1=t2, op=ALU.mult),
                          [ZC, T2], [ZC])
                    fw.op("act", lambda e, c=c: e.activation(out=zbuf[:, c, :], in_=zbuf[:, c, :], func=AF.Silu,
                                                            bias=cvec_sb[:, 8 + c:9 + c], scale=cvec_sb[:, 4 + c:5 + c]),
                          [ZC, V(None, cvec_tr)], [ZC])
                    fw.op("act", lambda e, c=c: e.activation(out=sqb, in_=zbuf[:, c, :], func=AF.Square), [ZC], [SQB])
                    fw.op("pe", lambda e, c=c, b3=b3: e.matmul(bank(b3)[:, :], lhsT=onesd_b[:], rhs=sqb,
                                                                start=(c == 0), stop=(c == 3)),
                          [CB, SQB], [V(None, b_tr[b3])], inc=True)
                fw.op("act", lambda e, b3=b3: e.activation(out=t2, in_=bank(b3)[:, :], func=AF.Ln, bias=EPS), [V(None, b_tr[b3])], [T2])
                fw.op("act", lambda e: e.activation(out=t2, in_=t2, func=AF.Exp, scale=-0.5), [T2], [T2])
                for c in range(4):
                    fw.op("dve", lambda e, c=c, sl=sl: e.scalar_tensor_tensor(
                        out=cvn[:, c, sl], in0=zbuf[:, c, :], scalar=cvec_sb[:, 12 + c:13 + c], in1=t2, op0=ALU.mult, op1=ALU.mult),
                        [V(None, z_tr[c]), V(None, cvec_tr), T2], [V(None, cvn_tr[c])])
                out_proj(s_wo2, cvn, lambda c: V(None, cvn_tr[c]), tiles=range(4 * sblk, 4 * sblk + 4))

            ckpt(4)
            ckpt(5)
            s_v = wload(win_piece(2))

            RS.new_phase()
            qT, qT_tr = RS.carve([128, 4, S], BF16)
            kT, kT_tr = RS.carve([128, 4, S], BF16)
            v1, v1_tr = RS.carve([128, NT, 8, 65], BF16)
            QT = V(None, qT_tr); KT = V(None, kT_tr); V1 = V(None, v1_tr)
            ebuf, ebuf_tr, ptb, ptb_tr = [], [], [], []
            for i in range(3):
                a_, t_ = RS.carve([128, 640], BF16); ebuf.append(a_); ebuf_tr.append(t_[0])
                a_, t_ = RS.carve([128, 640], BF16); ptb.append(a_); ptb_tr.append(t_[0])
            anb, anb_tr = RS.carve([128, 512], BF16)
            ANB = V(None, anb_tr)
            fw.op("dve", lambda e: e.memset(v1[:, :, :, 64:65], 1.0), [], [V1])

            proj_fm(s_q, lambda fc, sblk, bk: fw.op("act", lambda e: e.mul(
                out=qT[:, fc, sblk * 512:(sblk + 1) * 512], in_=bank(bk)[:, :], mul=0.125),
                [V(None, b_tr[bk])], [QT]))
            s_wo1 = wload(wout_piece(0))
            proj_fm(s_k, lambda fc, sblk, bk: fw.op("dve", lambda e: e.tensor_copy(
                out=kT[:, fc, sblk * 512:(sblk + 1) * 512], in_=bank(bk)[:, :]),
                [V(None, b_tr[bk])], [KT]))
            for stt in range(NT):
                bk = nbank()
                for dc in range(8):
                    fw.op("pe", lambda e, stt=stt, dc=dc, bk=bk: e.matmul(
                        bank(bk)[:, :], lhsT=hT[:, dc, stt * 128:(stt + 1) * 128], rhs=wslot[s_v][:, dc * 512:(dc + 1) * 512],
                        start=(dc == 0), stop=(dc == 7)),
                        [HT, V(None, wslot_tr[s_v])], [V(None, b_tr[bk])], inc=(dc == 7))
                copy(evac_eng(), v1[:, stt, :, 0:64], bank(bk)[:, :].rearrange("p (h d) -> p h d", d=64),
                     [V(None, b_tr[bk])], [V1])

            ckpt(6)
            RA.new_phase()
            anT, anT_tr = RA.carve([128, 4, S], BF16)
            ANT = V(None, anT_tr)
            ebias, ebias_tr = [], []
            for i in range(2):
                a_, t_ = RA.carve([128, 5, 640], BF16); ebias.append(a_); ebias_tr.append(t_[0])

            def load_ebias(h, l=l):
                i = h % 2
                fw.dma("pool", lambda e: e.dma_start(out=ebias[i], in_=eb_d[l * 8 + h].rearrange("p (t k) -> p t k", k=640)),
                       [], [V(None, ebias_tr[i])], f"ld_eb{i}")

            def exp_ebias(h):
                i = h % 2
                fw.op("act", lambda e: e.activation(out=ebias[i], in_=ebias[i], func=AF.Exp),
                      [V(None, ebias_tr[i])], [V(None, ebias_tr[i])])

            _ad = deps_of([wslot_tr[1], wslot_tr[2]])
            aout = RW.t[:, 6144:6144 + 8192].rearrange("p (t f) -> p t f", f=512)
            aout_tr = [Tr(_ad) for _ in range(NT)]
            RW.live += aout_tr
            load_ebias(0)
            load_ebias(1)
            exp_ebias(0)
            exp_ebias(1)
            pend_exp = None
            SA, SB_ = 0, 2
            step = 0
            pendq = []
            for h in range(8):
                hc, hp = h // 2, (h % 2) * 64
                EBv = V(None, ebias_tr[h % 2])
                for i in range(NT):
                    j0 = min(max(i - 2, 0), NT - 5)
                    ty = 0 if 2 <= i <= 13 else (1 if i == 0 else 2 if i == 1 else 3 if i == 14 else 4)
                    sbk = (step % 3) * 2
                    eb_i = step % 3
                    obk = 6 + step % 2
                    if i == 8 and pend_exp is not None:
                        exp_ebias(pend_exp); pend_exp = None
                    for m in range(5):
                        bk_ = sbk + (0 if m < 4 else 1)
                        col = (m % 4) * 128
                        fw.op("pe", lambda e, m=m, bk_=bk_, col=col, i=i, j0=j0, hc=hc, hp=hp: e.matmul(
                            bank(bk_)[:, col:col + 128], lhsT=kT[hp:hp + 64, hc, (j0 + m) * 128:(j0 + m + 1) * 128],
                            rhs=qT[hp:hp + 64, hc, i * 128:(i + 1) * 128], start=True, stop=True),
                            [KT, QT], [V(None, b_tr[bk_])], inc=(m == 3 or m == 4))
                    if len(pendq) >= 2:
                        pendq.pop(0)()
                    fw.op("act", lambda e, sbk=sbk, eb_i=eb_i: e.activation(out=ebuf[eb_i][:, 0:512], in_=bank(sbk)[:, :], func=AF.Exp),
                          [V(None, b_tr[sbk])], [V(None, ebuf_tr[eb_i])])
                    fw.op("act", lambda e, sbk=sbk, eb_i=eb_i: e.activation(out=ebuf[eb_i][:, 512:640], in_=bank(sbk + 1)[:, 0:128], func=AF.Exp),
                          [V(None, b_tr[sbk + 1])], [V(None, ebuf_tr[eb_i])])
                    fw.op("dve", lambda e, eb_i=eb_i, ty=ty, h=h: e.tensor_tensor(
                        out=ptb[eb_i], in0=ebuf[eb_i], in1=ebias[h % 2][:, ty, :], op=ALU.mult),
                        [V(None, ebuf_tr[eb_i]), EBv], [V(None, ptb_tr[eb_i])])

                    def pv(i=i, j0=j0, h=h, eb_i=eb_i, obk=obk):
                        for m in range(5):
                            fw.op("pe", lambda e, m=m: e.matmul(
                                bank(obk)[:, 0:65], lhsT=ptb[eb_i][:, m * 128:(m + 1) * 128], rhs=v1[:, j0 + m, h, :],
                                start=(m == 0), stop=(m == 4)),
                                [V(None, ptb_tr[eb_i]), V1], [V(None, b_tr[obk])], inc=(m == 4))
                        fw.op("dve", lambda e: e.reciprocal(out=sm_sb[:, eb_i:eb_i + 1], in_=bank(obk)[:, 64:65]),
                              [V(None, b_tr[obk])], [V(None, sm_tr)])
                        fw.op("dve", lambda e: e.tensor_scalar(out=aout[:, i, h * 64:(h + 1) * 64], in0=bank(obk)[:, 0:64],
                                                                scalar1=sm_sb[:, eb_i:eb_i + 1], scalar2=None, op0=ALU.mult),
                              [V(None, b_tr[obk]), V(None, sm_tr)], [V(None, aout_tr[i])])
                    pendq.append(pv)
                    step += 1
                if h + 2 < 8:
                    while pendq:
                        pendq.pop(0)()
                    load_ebias(h + 2)
                    pend_exp = h + 2
            while pendq:
                pendq.pop(0)()

            ckpt(7)
            for i, RSTD in norm_loop(512, lambda i: aout[:, i, :], lambda i: V(None, aout_tr[i])):
                fw.op("dve", lambda e, i=i: e.scalar_tensor_tensor(
                    out=anb, in0=aout[:, i, :], scalar=rstd_sb[:, i:i + 1], in1=aog_bc[:], op0=ALU.mult, op1=ALU.mult),
                    [V(None, aout_tr[i]), RSTD, V(None, aog_tr)], [ANB])
                bk = 6 + i % 2
                for c in range(4):
                    fw.op("pe", lambda e, c=c, bk=bk: e.transpose(
                        out=bankb(bk)[:, c * 128:(c + 1) * 128], in_=anb[:, c * 128:(c + 1) * 128], identity=ident_b[:]),
                        [ANB, CB], [V(None, b_tr[bk])], inc=(c == 3))
                fw.op("act", lambda e, i=i, bk=bk: e.copy(out=anT[:, :, i * 128:(i + 1) * 128],
                                                          in_=bankb(bk)[:, 0:512].rearrange("p (c s) -> p c s", s=128)),
                      [V(None, b_tr[bk])], [ANT])
            out_proj(s_wo1, anT, lambda c: ANT)

            ckpt(8)
            load_gain(n2g_d[l])
            RA.new_phase(); RS.new_phase(); RW.new_phase()
            h2, h2_tr = RA.carve([128, NT, D], BF16, ntr=NT)
            for stt, RSTD in norm_loop(D, lambda i: x_sb[:, i, :], lambda i: V(None, x_tr[i])):
                fw.op("dve", lambda e, stt=stt: e.scalar_tensor_tensor(
                    out=h2[:, stt, :], in0=x_sb[:, stt, :], scalar=rstd_sb[:, stt:stt + 1], in1=g_bc[:],
                    op0=ALU.mult, op1=ALU.mult),
                    [V(None, x_tr[stt]), RSTD, V(None, g_tr)], [V(None, h2_tr[stt])])

            fw.dma("sp", lambda e: e.dma_start(out=h2_d.rearrange("(t p) d -> p t d", p=128), in_=h2),
                   [V(None, h2_tr)], [V(None, H2D)], "st_h2")
            gp, gp_tr, up_, up_tr, dpf, dp_tr = [None] * 3, [None] * 3, [None] * 3, [None] * 3, [None] * 3, [None] * 3
            a_, t_ = RW.carve([128, 4096], BF16); gp[0] = a_; gp_tr[0] = t_[0]
            a_, t_ = RW.carve([128, 4096], BF16); up_[0] = a_; up_tr[0] = t_[0]
            a_, t_ = RW.carve([128, 4096], BF16); dpf[0] = a_; dp_tr[0] = t_[0]
            a_, t_ = RW.carve([128, 4096], BF16); gp[1] = a_; gp_tr[1] = t_[0]
            a_, t_ = RS.carve([128, 4096], BF16); up_[1] = a_; up_tr[1] = t_[0]
            NSL = 3
            NPC = FF // 512
            pieces = [(e_, p_) for e_ in range(NE) for p_ in range(NPC)]

            def issue_gu(k, l=l):
                if k >= len(pieces):
                    return
                e_, p_ = pieces[k]
                si = k % NSL
                fsl = slice(p_ * 512, (p_ + 1) * 512)
                fw.dma("pool", lambda e: e.dma_start(
                    out=gp[si].rearrange("p (c f) -> p c f", f=512),
                    in_=wg_d[l * NE + e_][:, fsl].rearrange("(c p) f -> p c f", p=128)), [], [V(None, gp_tr[si])], f"ld_e{si}_g")
                fw.dma("pool", lambda e: e.dma_start(
                    out=up_[si].rearrange("p (c f) -> p c f", f=512),
                    in_=wu_d[l * NE + e_][:, fsl].rearrange("(c p) f -> p c f", p=128)), [], [V(None, up_tr[si])], f"ld_e{si}_u")

            def issue_d(k, l=l):
                if k >= len(pieces):
                    return
                e_, p_ = pieces[k]
                si = k % NSL
                fw.dma("pool", lambda e: e.dma_start(
                    out=dpf[si].rearrange("p (c f) -> p c f", f=1024),
                    in_=wd_d[l * NE + e_][p_ * 512:(p_ + 1) * 512, :].rearrange("(c p) f -> p c f", p=128)),
                    [], [V(None, dp_tr[si])], f"ld_e{si}_d")

            issue_gu(0)
            issue_d(0)
            issue_gu(1)

            ckpt(9)
            h2T, h2T_tr = [], []
            h2T_off = (RS.off + 63) // 64 * 64
            for i in range(2):
                a_, t_ = RS.carve([128, 8, 128], BF16); h2T.append(a_); h2T_tr.append(t_[0])
            assert RS.off == h2T_off + 4096
            aff, aff_tr = RS.carve([128, NT, NE], F32); AFF = V(None, aff_tr)
            rs_, rs_tr = RS.carve([128, NT], F32); RSV = V(None, rs_tr)
            Mf, Mf_tr = RS.carve([128, NT, NE], F32); MF = V(None, Mf_tr)
            Mb, Mb_tr = RS.carve([128, NT, NE], BF16); MB = V(None, Mb_tr)
            mr, mr_tr = RS.carve([128, NT, NE], F32); MR = V(None, mr_tr)
            Gt, Gt_tr = RS.carve([128, NT, NE], F32); GT = V(None, Gt_tr)
            sel, sel_tr = RS.carve([128, NT, CAP], BF16); SEL = V(None, sel_tr)
            selT, selT_tr = [], []
            for i in range(2):
                a_, t_ = RS.carve([128, 2, S], BF16); selT.append(a_); selT_tr.append(t_[0])
            xsT, xsT_tr = RS.carve([128, 8, CAP], BF16); XST = V(None, xsT_tr)
            hid, hid_tr = [], []
            for i in range(2):
                a_, t_ = RS.carve([128, CAP], BF16); hid.append(a_); hid_tr.append(t_[0])
            sg, sg_tr = [], []
            for i in range(2):
                a_, t_ = RS.carve([128, CAP], F32); sg.append(a_); sg_tr.append(t_[0])
            eo_sb, eo_tr = RS.carve([128, 2, D], BF16); EO = V(None, eo_tr)
            stmp, stmp_tr = [], []
            for i in range(2):
                a_, t_ = RS.carve([128, 512], F32); stmp.append(a_); stmp_tr.append(t_[0])
            bis, bis_tr = RS.carve([128, 8], F32); BIS = V(None, bis_tr)
            gebf, gebf_tr = RS.carve([128, 8], BF16); GEB = V(None, gebf_tr)
            affT = selT[0][0:16, :, :].rearrange("p a s -> p (a s)").bitcast(F32)
            AFT = V(None, selT_tr[0])
            bjunk = selT[1][0:16, 0, :]
            BJ = V(None, selT_tr[1])
            maskT = selT[1][0:16, :, :].rearrange("p a s -> p (a s)").bitcast(F32)

            LB = 0
            for stt in range(NT):
                bk = 1 + stt % 2
                hb = stt % 2
                for dc in range(8):
                    fw.op("pe", lambda e, stt=stt, dc=dc, bk=bk: e.transpose(
                        out=bankb(bk)[:, dc * 128:(dc + 1) * 128], in_=h2[:, stt, dc * 128:(dc + 1) * 128], identity=ident_b[:]),
                        [V(None, h2_tr[stt]), CB], [V(None, b_tr[bk])], inc=(dc == 7))
                copy("act", h2T[hb], bankb(bk).rearrange("p (c s) -> p c s", s=128), [V(None, b_tr[bk])], [V(None, h2T_tr[hb])])
                for dc in range(8):
                    fw.op("pe", lambda e, stt=stt, dc=dc, hb=hb: e.matmul(
                        bank(LB)[:, stt * NE:(stt + 1) * NE], lhsT=h2T[hb][:, dc, :], rhs=wr_sb[:, dc, :],
                        start=(dc == 0), stop=(dc == 7)),
                        [V(None, h2T_tr[hb]), V(None, wr_tr)], [V(None, b_tr[LB])], inc=(dc == 7))
            fw.op("act", lambda e: e.activation(out=aff.rearrange("p t e -> p (t e)"), in_=bank(LB)[:, 0:NT * NE], func=AF.Exp),
                  [V(None, b_tr[LB])], [AFF])
            fw.op("dve", lambda e: e.tensor_reduce(out=rs_, in_=aff, axis=AX.X, op=ALU.add), [AFF], [RSV])
            fw.op("dve", lambda e: e.reciprocal(out=rs_, in_=rs_), [RSV], [RSV])
            fw.op("dve", lambda e: e.tensor_tensor(out=aff, in0=aff, in1=rs_.unsqueeze(2).to_broadcast([128, NT, NE]), op=ALU.mult),
                  [AFF, RSV], [AFF])
            for t in range(NT):
                bk = 4 + t // 4
                fw.op("pe", lambda e, t=t, bk=bk: e.transpose(
                    out=bank(bk)[0:16, (t % 4) * 128:(t % 4 + 1) * 128], in_=aff[:, t, :], identity=ident_f),
                    [AFF, CST], [V(None, b_tr[bk])], inc=(t % 4 == 3))
            for q4 in range(4):
                copy(evac_eng(), affT[:, q4 * 512:(q4 + 1) * 512], bank(4 + q4)[0:16, :], [V(None, b_tr[4 + q4])], [AFT])
            ckpt(10)
            affT8 = selT[0].rearrange("p a s -> p (a s)").bitcast(F32)
            junk8 = selT[1][:, 0, :]
            rep_tr = [Tr(deps_of([selT_tr[0]])) for _ in range(7)]
            for j_ in range(1, 8):
                fw.dma("sp", lambda e, j_=j_: e.dma_start(out=affT8[16 * j_:16 * j_ + 16, :], in_=affT8[0:16, :]),
                       [AFT], [V(None, rep_tr[j_ - 1])], f"ld_rep{j_}")
            REP = V(None, rep_tr)
            lo8, mid8, cnt8 = bis[:, 0:1], bis[:, 1:2], bis[:, 2:3]
            jvec = cst[:, 544:545]
            NPASS = 9
            fw.op("dve", lambda e: e.memset(bis[:, 0:1], 0.0), [], [BIS])
            fw.op("dve", lambda e: e.tensor_scalar(out=mid8, in0=jvec, scalar1=0.125, scalar2=None, op0=ALU.mult), [CST, BIS], [BIS])
            for ps in range(NPASS):
                stp = 8.0 ** -(ps + 1)
                fw.op("dve", lambda e: e.tensor_scalar(out=junk8, in0=affT8, scalar1=mid8, scalar2=None, op0=ALU.is_ge, op1=ALU.add,
                                                        accum_out=cnt8), [AFT, REP, BIS], [BJ, BIS])
                fw.op("dve", lambda e: e.tensor_single_scalar(out=gebf[:, 0:1], in_=cnt8, scalar=CAP - 0.5, op=ALU.is_ge), [BIS], [GEB])
                fw.op("pe", lambda e: e.matmul(bank(3)[:, 0:1], lhsT=modm_b[:], rhs=gebf[:, 0:1], start=True, stop=True),
                      [CB, GEB], [V(None, b_tr[3])])
                fw.op("dve", lambda e, stp=stp: e.scalar_tensor_tensor(out=lo8, in0=bank(3)[:, 0:1], scalar=stp, in1=lo8,
                                                                       op0=ALU.mult, op1=ALU.add), [V(None, b_tr[3]), BIS], [BIS])
                if ps + 1 < NPASS:
                    fw.op("dve", lambda e, stp=stp: e.scalar_tensor_tensor(out=mid8, in0=jvec, scalar=stp / 8.0, in1=lo8,
                                                                           op0=ALU.mult, op1=ALU.add), [CST, BIS], [BIS])
            lo = bis[0:16, 0:1]
            fw.op("dve", lambda e: e.tensor_scalar(out=maskT, in0=affT, scalar1=lo, scalar2=None, op0=ALU.is_ge),
                  [AFT, REP, BIS], [BJ, AFT])
            ckpt(11)
            for t in range(NT):
                fw.op("pe", lambda e, t=t: e.transpose(out=bank(1)[:, t * NE:(t + 1) * NE], in_=maskT[:, t * 128:(t + 1) * 128],
                                                       identity=cst[0:16, 0:16]),
                      [BJ, CST], [V(None, b_tr[1])], inc=(t == NT - 1))
            fw.op("dve", lambda e: e.tensor_copy(out=Mf.rearrange("p t e -> p (t e)"), in_=bank(1)[:, 0:NT * NE]), [V(None, b_tr[1])], [MF])
            fw.op("act", lambda e: e.copy(out=Mb.rearrange("p t e -> p (t e)"), in_=bank(1)[:, 0:NT * NE]), [V(None, b_tr[1])], [MB])
            for t in range(NT):
                for t2_ in range(t + 1):
                    fw.op("pe", lambda e, t=t, t2_=t2_: e.matmul(
                        bank(2)[:, t * NE:(t + 1) * NE], lhsT=(tri_b[:] if t2_ == t else ones_b[:]), rhs=Mb[:, t2_, :],
                        start=(t2_ == 0), stop=(t2_ == t)),
                        [CB, MB], [V(None, b_tr[2])], inc=(t2_ == t))
            fw.op("dve", lambda e: e.scalar_tensor_tensor(out=mr.rearrange("p t e -> p (t e)"), in0=bank(2)[:, 0:NT * NE], scalar=1.0,
                                                          in1=Mf.rearrange("p t e -> p (t e)"), op0=ALU.add, op1=ALU.mult),
                  [V(None, b_tr[2]), MF], [MR])
            fw.op("dve", lambda e: e.tensor_scalar(out=mr, in0=mr, scalar1=-1.0, scalar2=None, op0=ALU.add), [MR], [MR])
            fw.op("dve", lambda e: e.tensor_tensor(out=Gt, in0=aff, in1=Mf, op=ALU.mult), [AFF, MF], [GT])

            ckpt(12)
            pk = {"n": 0}

            def build_sel_tile(ex, t):
                fw.op("dve", lambda e: e.tensor_scalar(out=sel[:, t, :], in0=iota_c, scalar1=mr[:, t, ex:ex + 1],
                                                        scalar2=None, op0=ALU.is_equal),
                      [CST, MR], [SEL])

            _hd = deps_of(h2_tr)
            for j_, (lst, trl, si_) in enumerate(((dpf, dp_tr, 1), (gp, gp_tr, 2), (up_, up_tr, 2), (dpf, dp_tr, 2))):
                lst[si_] = RA.t[:, j_ * 4096:(j_ + 1) * 4096]
                trl[si_] = Tr(_hd)
                RA.live.append(trl[si_])
            issue_d(1)
            issue_gu(2)
            issue_d(2)
            _xd = deps_of(h2T_tr)
            xs_tm = RS.t[:, h2T_off // 2: h2T_off // 2 + 2048].rearrange("p (a d) -> p a d", d=D)
            xs_tr = [Tr(_xd), Tr(_xd)]
            RS.live += xs_tr

            def gather_a(ex):
                ib = ex % 2
                for ct in range(2):
                    for t in range(NT):
                        fw.op("pe", lambda e, t=t, ct=ct: e.matmul(
                            bank(6)[:, ct * 2:ct * 2 + 2], lhsT=sel[:, t, ct * 128:(ct + 1) * 128], rhs=tokc_b[:, t, :],
                            start=(t == 0), stop=(t == NT - 1)),
                            [SEL, CB], [V(None, b_tr[6])], inc=(t == NT - 1))
                fw.op("dve", lambda e: e.tensor_copy(out=idxf_sb[:, 0:4], in_=bank(6)[:, 0:4]), [V(None, b_tr[6])], [V(None, idxf_tr)])
                for ct in range(2):
                    fw.op("dve", lambda e, ct=ct: e.scalar_tensor_tensor(
                        out=idxf_sb[:, 4 + ct:5 + ct], in0=idxf_sb[:, ct * 2:ct * 2 + 1], scalar=128.0,
                        in1=idxf_sb[:, ct * 2 + 1:ct * 2 + 2], op0=ALU.mult, op1=ALU.add),
                        [V(None, idxf_tr)], [V(None, idxf_tr)])
                fw.op("dve", lambda e: e.tensor_copy(out=idx_sb[:, ib * 2:ib * 2 + 2], in_=idxf_sb[:, 4:6]),
                      [V(None, idxf_tr)], [V(None, idx_tr[ib])])
                for ct in range(2):
                    fw.dma("pool", lambda e, ct=ct: e.indirect_dma_start(
                        out=xs_tm[:, ct, :], out_offset=None, in_=h2_d[:, :],
                        in_offset=bass.IndirectOffsetOnAxis(ap=idx_sb[:, ib * 2 + ct:ib * 2 + ct + 1], axis=0)),
                        [V(None, idx_tr[ib]), V(None, H2D)], [V(None, xs_tr[ct])], f"ld_gx{ct}")

            def gather_b(ex):
                for ct in range(2):
                    bk = 6 + ct
                    for dc in range(8):
                        fw.op("pe", lambda e, ct=ct, dc=dc, bk=bk: e.transpose(
                            out=bankb(bk)[:, dc * 128:(dc + 1) * 128], in_=xs_tm[:, ct, dc * 128:(dc + 1) * 128], identity=ident_b[:]),
                            [V(None, xs_tr[ct]), CB], [V(None, b_tr[bk])], inc=(dc == 7))
                    copy(evac_eng(), xsT[:, :, ct * 128:(ct + 1) * 128], bankb(bk).rearrange("p (c s) -> p c s", s=128),
                         [V(None, b_tr[bk])], [XST])

            def make_selT(ex):
                sT = selT[ex % 2]; sT_tr = selT_tr[ex % 2]
                for ct in range(2):
                    for t in range(NT):
                        bk = 6 + t // 8
                        fw.op("pe", lambda e, t=t, ct=ct, bk=bk: e.transpose(
                            out=bankb(bk)[:, (t % 8) * 128:(t % 8 + 1) * 128], in_=sel[:, t, ct * 128:(ct + 1) * 128], identity=ident_b[:]),
                            [SEL, CB], [V(None, b_tr[bk])], inc=(t % 8 == 7))
                    for hh in range(2):
                        copy(evac_eng(), sT[:, ct, hh * 1024:(hh + 1) * 1024], bankb(6 + hh), [V(None, b_tr[6 + hh])], [V(None, sT_tr)])

            def eo_mm(fcg, si, hb, first, last):
                dbuf = dpf[si][:, (fcg // 2) * 2048:(fcg // 2 + 1) * 2048]
                dtr = dp_tr[si]
                for ct in range(2):
                    for dh in range(2):
                        bk = 2 + ct * 2 + dh
                        fw.op("pe", lambda e, ct=ct, dh=dh, bk=bk: e.matmul(
                            bank(bk)[:, :], lhsT=hid[hb][:, ct * 128:(ct + 1) * 128],
                            rhs=dbuf[:, (fcg % 2) * 1024 + dh * 512: (fcg % 2) * 1024 + (dh + 1) * 512],
                            start=first, stop=last),
                            [V(None, hid_tr[hb]), V(None, dtr)], [V(None, b_tr[bk])], inc=last or (ct == 1 and dh == 1))

            def ffn(ex, hook):
                pend_eo = None
                nchunk = 0
                for p_ in range(NPC):
                    kpiece = pk["n"]
                    si = kpiece % NSL
                    for fcg in range(4):
                        hb = nchunk % 2
                        for (wbuf, wtr, colo) in ((gp[si], gp_tr[si], 0), (up_[si], up_tr[si], 256)):
                            for dc in range(8):
                                fw.op("pe", lambda e, dc=dc, wbuf=wbuf, colo=colo, hb=hb, fcg=fcg: e.matmul(
                                    bank(hb)[:, colo:colo + 256],
                                    lhsT=wbuf[:, dc * 512 + fcg * 128: dc * 512 + (fcg + 1) * 128],
                                    rhs=xsT[:, dc, :], start=(dc == 0), stop=(dc == 7)),
                                    [V(None, wtr), XST], [V(None, b_tr[hb])], inc=(dc == 7))
                        if fcg == 3:
                            issue_gu(kpiece + NSL)
                        if pend_eo is not None:
                            pend_eo()
                        fw.op("act", lambda e, hb=hb: e.activation(out=sg[hb], in_=bank(hb)[:, 0:256], func=AF.Silu),
                              [V(None, b_tr[hb])], [V(None, sg_tr[hb])])
                        fw.op("dve", lambda e, hb=hb: e.tensor_tensor(out=hid[hb], in0=bank(hb)[:, 256:512], in1=sg[hb], op=ALU.mult),
                              [V(None, b_tr[hb]), V(None, sg_tr[hb])], [V(None, hid_tr[hb])])
                        hook(nchunk)
                        first = (nchunk == 0)
                        last = (nchunk == 4 * NPC - 1)

                        def mk(fcg=fcg, si=si, hb=hb, first=first, last=last, rel=(fcg == 3), kpiece=kpiece):
                            def f():
                                eo_mm(fcg, si, hb, first, last)
                                if rel:
                                    issue_d(kpiece + NSL)
                            return f
                        pend_eo = mk()
                        nchunk += 1
                    pk["n"] += 1
                pend_eo()

            def eo_evac():
                for ct in range(2):
                    for dh in range(2):
                        bk = 2 + ct * 2 + dh
                        copy(evac_eng(), eo_sb[:, ct, dh * 512:(dh + 1) * 512], bank(bk)[:, :], [V(None, b_tr[bk])], [EO])

            def scatter(ex):
                sT = selT[ex % 2]; sT_tr = selT_tr[ex % 2]
                for stt in range(NT):
                    for dh in range(2):
                        bk = (6, 7, 0, 1, 2, 3, 4, 5)[(stt * 2 + dh) % 8]
                        for ct in range(2):
                            fw.op("pe", lambda e, stt=stt, dh=dh, ct=ct, bk=bk: e.matmul(
                                bank(bk)[:, :], lhsT=sT[:, ct, stt * 128:(stt + 1) * 128], rhs=eo_sb[:, ct, dh * 512:(dh + 1) * 512],
                                start=(ct == 0), stop=(ct == 1)),
                                [V(None, sT_tr), EO], [V(None, b_tr[bk])], inc=(ct == 1))
                        XH = V(None, x_tr[stt][dh])
                        if dh == 0:
                            fw.op("dve", lambda e, stt=stt, dh=dh, bk=bk: e.scalar_tensor_tensor(
                                out=x_sb[:, stt, dh * 512:(dh + 1) * 512], in0=bank(bk)[:, :], scalar=Gt[:, stt, ex:ex + 1],
                                in1=x_sb[:, stt, dh * 512:(dh + 1) * 512], op0=ALU.mult, op1=ALU.add),
                                [V(None, b_tr[bk]), GT, XH], [XH])
                        else:
                            k_ = stt % 2
                            fw.op("act", lambda e, stt=stt, bk=bk, k_=k_: e.activation(
                                out=stmp[k_], in_=bank(bk)[:, :], func=AF.Copy, scale=Gt[:, stt, ex:ex + 1]),
                                [V(None, b_tr[bk]), GT], [V(None, stmp_tr[k_])])
                            fw.op("pool", lambda e, stt=stt, dh=dh, k_=k_: e.tensor_tensor(
                                out=x_sb[:, stt, dh * 512:(dh + 1) * 512], in0=x_sb[:, stt, dh * 512:(dh + 1) * 512],
                                in1=stmp[k_], op=ALU.add),
                                [V(None, stmp_tr[k_]), XH], [XH])

            for t in range(NT):
                build_sel_tile(0, t)
            gather_a(0)
            make_selT(0)
            gather_b(0)
            def mk_hook(nxt):
                def hook(n):
                    if nxt >= NE:
                        return
                    if n < 8:
                        build_sel_tile(nxt, 2 * n)
                        build_sel_tile(nxt, 2 * n + 1)
                    elif n == 8:
                        gather_a(nxt)
                return hook

            for ex in range(NE):
                nxt = ex + 1
                ffn(ex, mk_hook(nxt))
                eo_evac()
                if nxt < NE:
                    make_selT(nxt)
                scatter(ex)
                if nxt < NE:
                    gather_b(nxt)

          except _Stop:
            pass
        load_gain(fg_d[0])
        for stt, RSTD in norm_loop(D, lambda i: x_sb[:, i, :], lambda i: V(None, x_tr[i])):
            fw.op("dve", lambda e, stt=stt: e.scalar_tensor_tensor(
                out=x_sb[:, stt, :], in0=x_sb[:, stt, :], scalar=rstd_sb[:, stt:stt + 1], in1=g_bc[:], op0=ALU.mult, op1=ALU.mult),
                [V(None, x_tr[stt]), RSTD, V(None, g_tr)], [V(None, x_tr[stt])])
        for t4 in range(4):
            fw.dma("sp", lambda e, t4=t4: e.dma_start(
                out=out_d[t4 * 512:(t4 + 1) * 512, :].rearrange("(t p) d -> p t d", p=128),
                in_=x_sb[:, t4 * 4:(t4 + 1) * 4, :]),
                [V(None, xflat(t4 * 4, (t4 + 1) * 4))], [], f"st_o{t4}")
        fw.final_wait("sp", [f"st_o{t4}" for t4 in range(4)])
        fw.emit()
    return nc


def _ebias_table(rpb):
    L = rpb.shape[0]
    p = np.arange(128)
    ka, kc = p // 64, p % 64
    qa, qc = p // 64, p % 64
    out = np.empty((L, 8, 128, 5, 5, 128), np.float32)
    win_start = np.clip(qc - 8, 0, 48)
    colmask = (kc[:, None] >= win_start[None, :]) & (kc[:, None] < win_start[None, :] + 16)
    coloff = np.clip(kc[:, None] - qc[None, :] + 15, 0, 30)
    for ty, i in enumerate((2, 0, 1, 14, 15)):
        j0 = min(max(i - 2, 0), 11)
        for m in range(5):
            kr = 2 * (j0 + m) + ka
            qr = 2 * i + qa
            rs = np.clip(qr - 4, 0, 24)
            rowmask = (kr[:, None] >= rs[None, :]) & (kr[:, None] < rs[None, :] + 8)
            rowoff = np.clip(kr[:, None] - qr[None, :] + 7, 0, 14)
            vals = rpb[:, :, rowoff, coloff]
            out[:, :, :, ty, m, :] = np.where((rowmask & colmask)[None, None], vals, np.float32(NEG))
    return np.ascontiguousarray(out.reshape(L * 8, 128, 3200))


def _consts():
    c = np.zeros((128, 704), np.float32)
    c[:, 0:128] = np.eye(128, dtype=np.float32)
    k = np.arange(128)
    c[:, 128:256] = (k[:, None] < k[None, :]).astype(np.float32)
    c[:, 256:512] = np.arange(256, dtype=np.float32)[None, :]
    c[:, 512:544:2] = np.arange(16, dtype=np.float32)[None, :]
    c[:, 513:544:2] = k[:, None].astype(np.float32)
    c[:, 544] = (k // 16 + 1).astype(np.float32)
    c[:, 576:704] = ((k[:, None] % 16) == (k[None, :] % 16)).astype(np.float32)
    return c


def prep(inputs, nl=DEPTH):
    f = lambda a: np.ascontiguousarray(np.asarray(a, dtype=np.float32))
    conv_w = f(inputs["conv_w"])
    cw = conv_w.reshape(DEPTH, 31, 4, 128).transpose(0, 3, 2, 1).reshape(DEPTH, 128, 124)
    vecs = np.stack([f(inputs["conv_b"]), f(inputs["conv_ln_g"]), f(inputs["conv_ln_b"]), f(inputs["conv_out_g"])], 1)
    cvec = vecs.reshape(DEPTH, 4, 4, 128).transpose(0, 3, 1, 2).reshape(DEPTH, 128, 16)
    shared = {
        "norm1_g": f(inputs["norm1_g"]), "norm2_g": f(inputs["norm2_g"]), "final_g": f(inputs["final_g"]).reshape(1, D),
        "attn_out_g": f(inputs["attn_out_g"]), "w_in": f(inputs["w_in"]), "w_out": f(inputs["w_out"]),
        "w_router": f(inputs["w_router"]),
        "w_gate": f(inputs["w_gate"]).reshape(DEPTH * NE, D, FF), "w_up": f(inputs["w_up"]).reshape(DEPTH * NE, D, FF),
        "w_down": f(inputs["w_down"]).reshape(DEPTH * NE, FF, D),
        "cw": np.ascontiguousarray(cw), "cvec": np.ascontiguousarray(cvec),
        "ebias": _ebias_table(f(inputs["rpb"])), "consts": _consts(),
    }
    return shared


_NC_CACHE = {}


def kernel(**inputs):
    x = np.ascontiguousarray(np.asarray(inputs["x"], dtype=np.float32))
    B = x.shape[0]
    shared = prep(inputs)
    if DEPTH not in _NC_CACHE:
        _NC_CACHE[DEPTH] = build(DEPTH)
    nc = _NC_CACHE[DEPTH]
    in_maps = [dict(shared, x=x[b]) for b in range(B)]
    res = run_bass_kernel_spmd(nc, in_maps, core_ids=list(range(B)))
    return np.stack([res.results[b]["out"] for b in range(B)], 0).astype(np.float32)
```

```python
import contextlib
import os
import numpy as np
import concourse.bass as bass
import concourse.mybir as mybir
from concourse.bass_utils import run_bass_kernel_spmd

F32 = mybir.dt.float32
BF16 = mybir.dt.bfloat16
AF = mybir.ActivationFunctionType
ALU = mybir.AluOpType
AX = mybir.AxisListType

D = 1024
S = 2048
DEPTH = 4
NT = 16
NE = 16
CAP = 256
FF = 2048
EPS = 1e-6
NEG = -30000.0
NBIS = 26
STOP = int(os.environ.get("KSTOP", "99"))


class _Stop(Exception):
    pass


def ckpt(k):
    if STOP == k:
        raise _Stop()


class Tr:
    __slots__ = ("w", "r", "psum")

    def __init__(self, deps=None, psum=False):
        self.w = None
        self.r = dict(deps) if deps else {}
        self.psum = psum


class V:
    __slots__ = ("ap", "trs")

    def __init__(self, ap, trs):
        self.ap = ap
        self.trs = trs if isinstance(trs, (list, tuple)) else [trs]


class Eng:
    def __init__(self, key):
        self.key = key
        self.cnt = 0
        self.seen = {}
        self.q = []


class FW:
    def __init__(self, nc, stack):
        self.nc = nc
        self.stack = stack
        self.sems = {}
        self.eng = {}
        for k in ("pe", "act", "dve", "pool", "sp"):
            self.eng[k] = Eng(k)
            self.sems[k] = stack.enter_context(nc.semaphore("s_" + k))
        self.dval = {}

    def dsem(self, key):
        if key not in self.sems:
            self.sems[key] = self.stack.enter_context(self.nc.semaphore("d_" + key))
            self.dval[key] = 0
        return key

    def _waits(self, E, needs):
        for k, val in needs.items():
            if val <= 0 or E.seen.get(k, 0) >= val:
                continue
            if k in self.eng and val > self.eng[k].cnt:
                raise RuntimeError(f"dangling wait: {E.key} needs {k}>={val} (cnt={self.eng[k].cnt})")
            E.seen[k] = val
            E.q.append(("wait", k, val))

    def _needs(self, E, reads, writes):
        needs = {}

        def need(k, v, war=False):
            if k == E.key and (E.key == "pe" or war):
                return
            if v > needs.get(k, 0):
                needs[k] = v

        for v_ in reads:
            for t in v_.trs:
                if t.w:
                    need(*t.w)
                if t.psum:
                    for k, val in t.r.items():
                        need(k, val, war=True)
        for v_ in writes:
            for t in v_.trs:
                if t.w:
                    need(*t.w)
                for k, val in t.r.items():
                    need(k, val, war=True)
        return needs

    def op(self, en, fn, reads=(), writes=(), inc=True):
        E = self.eng[en]
        self._waits(E, self._needs(E, reads, writes))
        if inc:
            E.cnt += 1
            mark = E.cnt
        else:
            mark = E.cnt + 1
        E.q.append(("op", fn, inc))
        for v_ in reads:
            for t in v_.trs:
                if t.r.get(E.key, 0) < mark:
                    t.r[E.key] = mark
        for v_ in writes:
            for t in v_.trs:
                t.w = (E.key, mark)
                t.r = {}

    def dma(self, en, fn, reads, writes, dsem):
        E = self.eng[en]
        self.dsem(dsem)
        needs = self._needs(E, reads, writes)
        if self.dval[dsem] > needs.get(dsem, 0):
            needs[dsem] = self.dval[dsem]
        self._waits(E, needs)
        self.dval[dsem] += 16
        val = self.dval[dsem]
        E.q.append(("dma", fn, dsem))
        for v_ in reads:
            for t in v_.trs:
                t.r[dsem] = val
        for v_ in writes:
            for t in v_.trs:
                t.w = (dsem, val)
                t.r = {}

    def final_wait(self, en, keys):
        self._waits(self.eng[en], {k: self.dval[k] for k in keys})

    def emit(self):
        for E in self.eng.values():
            assert E.cnt < 60000, (E.key, E.cnt)
        for k, v in self.dval.items():
            assert v < 60000, (k, v)
        with self.nc.Block() as block:
            def run(E):
                def body(e):
                    for it in E.q:
                        if it[0] == "wait":
                            e.wait_ge(self.sems[it[1]], it[2])
                        elif it[0] == "op":
                            ins = it[1](e)
                            if it[2]:
                                ins.then_inc(self.sems[E.key], 1)
                        else:
                            it[1](e).then_inc(self.sems[it[2]], 16)
                return body
            block.sync(run(self.eng["sp"]))
            block.tensor(run(self.eng["pe"]))
            block.scalar(run(self.eng["act"]))
            block.vector(run(self.eng["dve"]))
            block.gpsimd(run(self.eng["pool"]))


def deps_of(trs):
    d = {}
    for t in trs:
        if t.w and t.w[1] > d.get(t.w[0], 0):
            d[t.w[0]] = t.w[1]
        for k, v in t.r.items():
            if v > d.get(k, 0):
                d[k] = v
    return d


class Region:
    def __init__(self, nc, stack, name, nbytes):
        self.t = stack.enter_context(nc.sbuf_tensor(name, [128, nbytes // 2], BF16))
        self.nbytes = nbytes
        self.live = []
        self.base = {}
        self.off = 0

    def new_phase(self):
        self.base = deps_of(self.live + [Tr(self.base)])
        self.live = []
        self.off = 0

    def carve(self, shape, dtype, ntr=1, at=None):
        es = 4 if dtype == F32 else 2
        n = int(np.prod(shape[1:]))
        nb = n * es
        off = self.off if at is None else at
        off = (off + 63) // 64 * 64
        assert off + nb <= self.nbytes, (self.t, off, nb, self.nbytes)
        if at is None:
            self.off = off + nb
        ap = self.t[0:shape[0], off // 2: (off + nb) // 2]
        if dtype != BF16:
            ap = ap.bitcast(dtype)
        if len(shape) == 3:
            ap = ap.rearrange("p (a b) -> p a b", b=shape[2])
        elif len(shape) == 4:
            ap = ap.rearrange("p (a b c) -> p a b c", b=shape[2], c=shape[3])
        trs = [Tr(self.base) for _ in range(ntr)]
        self.live += trs
        return ap, trs


def build(nl=DEPTH):
    nc = bass.Bass("TRN2", target_bir_lowering=False)

    def din(name, shape, dt=F32):
        return nc.dram_tensor(name, shape, dt, kind="ExternalInput").ap()

    x_d = din("x", [S, D])
    n1g_d = din("norm1_g", [DEPTH, D])
    n2g_d = din("norm2_g", [DEPTH, D])
    fg_d = din("final_g", [1, D])
    aog_d = din("attn_out_g", [DEPTH, 512])
    win_d = din("w_in", [DEPTH, D, 2560])
    wout_d = din("w_out", [DEPTH, D, D])
    wr_d = din("w_router", [DEPTH, D, NE])
    wg_d = din("w_gate", [DEPTH * NE, D, FF])
    wu_d = din("w_up", [DEPTH * NE, D, FF])
    wd_d = din("w_down", [DEPTH * NE, FF, D])
    cw_d = din("cw", [DEPTH, 128, 124])
    cvec_d = din("cvec", [DEPTH, 128, 16])
    eb_d = din("ebias", [DEPTH * 8, 128, 3200])
    cst_d = din("consts", [128, 704])
    out_d = nc.dram_tensor("out", [S, D], F32, kind="ExternalOutput").ap()
    h2_d = nc.dram_tensor("h2_scr", [S, D], BF16, kind="Internal").ap()

    with contextlib.ExitStack() as st:
        fw = FW(nc, st)
        sb = lambda n, s, d: st.enter_context(nc.sbuf_tensor(n, s, d))
        NOTR = []

        x_sb = sb("x_sb", [128, NT, D], F32)
        x_tr = [[Tr(), Tr()] for _ in range(NT)]
        xflat = lambda a, b: [t_ for pair in x_tr[a:b] for t_ in pair]
        RA = Region(nc, st, "RA", 32768)
        RS = Region(nc, st, "RS", 61440)
        RW = Region(nc, st, "RW", 36864)
        cst = sb("cst", [128, 704], F32); cst_tr = Tr()
        ident_f = cst[:, 0:128]
        iota_c = cst[:, 256:512]
        ident_b = sb("ident_b", [128, 128], BF16)
        tri_b = sb("tri_b", [128, 128], BF16)
        ones_b = sb("ones_b", [128, 128], BF16)
        onesd_b = sb("onesd_b", [128, 128], BF16)
        cb_tr = Tr()
        tokc_b = sb("tokc_b", [128, NT, 2], BF16)
        modm_b = sb("modm_b", [128, 128], BF16)
        idx_sb = sb("idx_sb", [128, 4], mybir.dt.int32); idx_tr = [Tr(), Tr()]
        H2D = Tr()
        idxf_sb = sb("idxf_sb", [128, 8], F32); idxf_tr = Tr()
        g_bc = sb("g_bc", [128, D], F32); g_tr = Tr()
        aog_bc = sb("aog_bc", [128, 512], F32); aog_tr = Tr()
        cw_sb = sb("cw_sb", [128, 124], F32); cw_tr = Tr()
        cvec_sb = sb("cvec_sb", [128, 16], F32); cvec_tr = Tr()
        wr_sb = sb("wr_sb", [128, 8, NE], BF16); wr_tr = Tr()
        ss_sb = sb("ss_sb", [128, NT], F32); ss_tr = [Tr() for _ in range(4)]
        rstd_sb = sb("rstd_sb", [128, NT], F32); rstd_tr = [Tr() for _ in range(4)]
        sm_sb = sb("sm_sb", [128, 64], F32); sm_tr = Tr()
        junk_sb = sb("junk_sb", [128, D], BF16); junk_tr = Tr()

        banks = [st.enter_context(nc.psum_tensor(f"bank{i}", [128, 512], F32)) for i in range(8)]
        b_tr = [Tr(psum=True) for _ in range(8)]

        def bank(i):
            return banks[i]

        def bankb(i):
            return banks[i][:].bitcast(BF16)

        rr = {"ev": 0}

        def evac_eng():
            rr["ev"] ^= 1
            return "act" if rr["ev"] else "dve"

        def copy(en, out, in_, reads, writes):
            if en == "act":
                fw.op("act", lambda e: e.copy(out=out, in_=in_), reads, writes)
            else:
                fw.op(en, lambda e: e.tensor_copy(out=out, in_=in_), reads, writes)

        fw.dma("sp", lambda e: e.dma_start(out=cst[:], in_=cst_d[:, :]), [], [V(None, cst_tr)], "ld_c")
        for t4 in range(4):
            fw.dma("sp", lambda e, t4=t4: e.dma_start(
                out=x_sb[:, t4 * 4:(t4 + 1) * 4, :],
                in_=x_d[t4 * 512:(t4 + 1) * 512, :].rearrange("(t p) d -> p t d", p=128)),
                [], [V(None, xflat(t4 * 4, (t4 + 1) * 4))], f"ld_x{t4}")
        fw.op("dve", lambda e: e.tensor_copy(out=ident_b[:], in_=cst[:, 0:128]), [V(None, cst_tr)], [V(None, cb_tr)])
        fw.op("dve", lambda e: e.tensor_copy(out=tri_b[:], in_=cst[:, 128:256]), [V(None, cst_tr)], [V(None, cb_tr)])
        fw.op("dve", lambda e: e.memset(ones_b[:], 1.0), [], [V(None, cb_tr)])
        fw.op("dve", lambda e: e.tensor_copy(out=modm_b[:], in_=cst[:, 576:704]), [V(None, cst_tr)], [V(None, cb_tr)])
        fw.op("dve", lambda e: e.tensor_copy(out=tokc_b[:].rearrange("p t k -> p (t k)"), in_=cst[:, 512:544]), [V(None, cst_tr)], [V(None, cb_tr)])
        fw.op("dve", lambda e: e.memset(onesd_b[:], 1.0 / 512.0), [], [V(None, cb_tr)])
        CB = V(None, cb_tr)
        CST = V(None, cst_tr)

        def load_gain(src_row_ap):
            fw.dma("sp", lambda e: e.dma_start(out=g_bc[:], in_=src_row_ap.partition_broadcast(128)),
                   [], [V(None, g_tr)], "ld_g")

        def norm_loop(width, get_in, in_reads):
            def stats(g):
                lo_, hi_ = 4 * g, 4 * g + 4
                for i in range(lo_, hi_):
                    fw.op("act", lambda e, i=i: e.activation(out=junk_sb[:, 0:width], in_=get_in(i), func=AF.Square,
                                                            accum_out=ss_sb[:, i:i + 1]),
                          [in_reads(i)], [V(None, junk_tr), V(None, ss_tr[g])])
                fw.op("act", lambda e: e.activation(out=rstd_sb[:, lo_:hi_], in_=ss_sb[:, lo_:hi_], func=AF.Ln,
                                                    bias=EPS, scale=1.0 / width),
                      [V(None, ss_tr[g])], [V(None, rstd_tr[g])])
                fw.op("act", lambda e: e.activation(out=rstd_sb[:, lo_:hi_], in_=rstd_sb[:, lo_:hi_], func=AF.Exp, scale=-0.5),
                      [V(None, rstd_tr[g])], [V(None, rstd_tr[g])])
            stats(0)
            for g in range(4):
                if g + 1 < 4:
                    stats(g + 1)
                for i in range(4 * g, 4 * g + 4):
                    yield i, V(None, rstd_tr[g])

        for l in range(nl if STOP == 99 else 1):
          try:
            load_gain(n1g_d[l])
            fw.dma("sp", lambda e, l=l: e.dma_start(out=aog_bc[:], in_=aog_d[l].partition_broadcast(128)), [], [V(None, aog_tr)], "ld_p")
            fw.dma("sp", lambda e, l=l: e.dma_start(out=cw_sb[:], in_=cw_d[l]), [], [V(None, cw_tr)], "ld_p")
            fw.dma("sp", lambda e, l=l: e.dma_start(out=cvec_sb[:], in_=cvec_d[l]), [], [V(None, cvec_tr)], "ld_p")
            fw.dma("pool", lambda e, l=l: e.dma_start(out=wr_sb[:], in_=wr_d[l].rearrange("(c p) e -> p c e", p=128)),
                   [], [V(None, wr_tr)], "ld_wr")

            RA.new_phase(); RS.new_phase(); RW.new_phase()
            hT, hT_tr = RA.carve([128, 8, S], BF16, ntr=NT)
            HT = V(None, hT_tr)
            htile, htile_tr = [], []
            for i in range(2):
                a_, t_ = RS.carve([128, D], BF16)
                htile.append(a_); htile_tr.append(t_[0])
            wslot, wslot_tr = [], []
            for i in range(3):
                a_, t_ = RW.carve([128, 6144], BF16)
                wslot.append(a_); wslot_tr.append(t_[0])
            wq = {"n": 0}

            def wload(fns):
                si = wq["n"] % 3
                wq["n"] += 1
                for f in fns:
                    o, i_ = f(wslot[si])
                    fw.dma("pool", lambda e, o=o, i_=i_: e.dma_start(out=o, in_=i_), [], [V(None, wslot_tr[si])], f"ld_w{si}")
                return si

            def win_piece(cb, l=l):
                return [lambda sl: (sl[:, 0:4096].rearrange("p (c f) -> p c f", f=512),
                                    win_d[l][:, cb * 512:(cb + 1) * 512].rearrange("(c p) f -> p c f", p=128))]

            def wout_piece(half, l=l):
                return [lambda sl: (sl[:, 0:4096].rearrange("p (c f) -> p c f", f=1024),
                                    wout_d[l][half * 512:(half + 1) * 512, :].rearrange("(c p) f -> p c f", p=128))]

            s_g = wload(win_piece(4))
            s_a = wload(win_piece(3))

            for stt, RSTD in norm_loop(D, lambda i: x_sb[:, i, :], lambda i: V(None, x_tr[i])):
                hb = stt % 2
                fw.op("dve", lambda e, stt=stt, hb=hb: e.scalar_tensor_tensor(
                    out=htile[hb], in0=x_sb[:, stt, :], scalar=rstd_sb[:, stt:stt + 1], in1=g_bc[:],
                    op0=ALU.mult, op1=ALU.mult),
                    [V(None, x_tr[stt]), RSTD, V(None, g_tr)], [V(None, htile_tr[hb])])
                bk = stt % 2
                for dc in range(8):
                    fw.op("pe", lambda e, dc=dc, hb=hb, bk=bk: e.transpose(
                        out=bankb(bk)[:, dc * 128:(dc + 1) * 128], in_=htile[hb][:, dc * 128:(dc + 1) * 128], identity=ident_b[:]),
                        [V(None, htile_tr[hb]), CB], [V(None, b_tr[bk])], inc=(dc == 7))
                copy("act", hT[:, :, stt * 128:(stt + 1) * 128], bankb(bk).rearrange("p (c s) -> p c s", s=128),
                     [V(None, b_tr[bk])], [V(None, hT_tr[stt])])

            ckpt(1)
            RS.new_phase()
            upad, upad_tr = RS.carve([128, 4, S + 30], BF16)
            UP = V(None, upad_tr)
            diag, diag_tr = RS.carve([128, 31, 128], BF16, ntr=2)
            cvn, cvn_tr = RS.carve([128, 4, S], BF16, ntr=4)
            zbuf, z_tr = RS.carve([128, 4, 512], F32, ntr=4)
            t1, t1_tr = RS.carve([128, 512], F32)
            t2, t2_tr = RS.carve([128, 512], F32)
            sqb2, sqb2_tr = [], []
            for i_ in range(2):
                a_, t_ = RS.carve([128, 512], BF16); sqb2.append(a_); sqb2_tr.append(t_[0])
            sqn = {"n": 0}
            T1 = V(None, t1_tr); T2 = V(None, t2_tr)
            fw.op("dve", lambda e: e.memset(upad[:, :, 0:15], 0.0), [], [UP])
            fw.op("dve", lambda e: e.memset(upad[:, :, S + 15:S + 30], 0.0), [], [UP])

            bkc = {"n": 0}

            def nbank(lo=0, hi=8):
                b = lo + bkc["n"] % (hi - lo)
                bkc["n"] += 1
                return b

            def proj_fm(slot, evac):
                for fc in range(4):
                    for sblk in range(4):
                        bk = nbank()
                        for dc in range(8):
                            fw.op("pe", lambda e, fc=fc, sblk=sblk, dc=dc, bk=bk, slot=slot: e.matmul(
                                bank(bk)[:, :], lhsT=wslot[slot][:, dc * 512 + fc * 128: dc * 512 + (fc + 1) * 128],
                                rhs=hT[:, dc, sblk * 512:(sblk + 1) * 512], start=(dc == 0), stop=(dc == 7)),
                                [V(None, wslot_tr[slot]), HT], [V(None, b_tr[bk])], inc=(dc == 7))
                        evac(fc, sblk, bk)

            proj_fm(s_g, lambda fc, sblk, bk: fw.op("act", lambda e: e.activation(
                out=upad[:, fc, 15 + sblk * 512: 15 + (sblk + 1) * 512], in_=bank(bk)[:, :], func=AF.Sigmoid),
                [V(None, b_tr[bk])], [UP]))
            s_wo2 = wload(wout_piece(1))
            proj_fm(s_a, lambda fc, sblk, bk: fw.op("dve", lambda e: e.tensor_tensor(
                out=upad[:, fc, 15 + sblk * 512: 15 + (sblk + 1) * 512], in0=bank(bk)[:, :],
                in1=upad[:, fc, 15 + sblk * 512: 15 + (sblk + 1) * 512], op=ALU.mult),
                [V(None, b_tr[bk]), UP], [UP]))
            s_q = wload(win_piece(0))
            s_k = wload(win_piece(1))

            ckpt(2)
            for c in range(4):
                for (j0_, j1_, hf_) in ((0, 16, 0), (16, 31, 1)):
                    fw.op("dve", lambda e, c=c, j0_=j0_, j1_=j1_: e.tensor_tensor(
                        out=diag[:, j0_:j1_, :], in0=ident_b[:].unsqueeze(1).to_broadcast([128, j1_ - j0_, 128]),
                        in1=cw_sb[:, c * 31 + j0_:c * 31 + j1_].unsqueeze(2).to_broadcast([128, j1_ - j0_, 128]), op=ALU.mult),
                        [CB, V(None, cw_tr)], [V(None, diag_tr[hf_])])
                for sblk in range(4):
                    bk = nbank()
                    for j in range(31):
                        fw.op("pe", lambda e, c=c, j=j, sblk=sblk, bk=bk: e.matmul(
                            bank(bk)[:, :], lhsT=diag[:, j, :], rhs=upad[:, c, sblk * 512 + j: sblk * 512 + j + 512],
                            start=(j == 0), stop=(j == 30)),
                            [V(None, diag_tr[j // 16]), UP], [V(None, b_tr[bk])], inc=(j == 30 or j == 15))
                    fw.op("act", lambda e, c=c, sblk=sblk, bk=bk: e.activation(
                        out=cvn[:, c, sblk * 512:(sblk + 1) * 512], in_=bank(bk)[:, :], func=AF.Identity,
                        bias=cvec_sb[:, c:c + 1]),
                        [V(None, b_tr[bk]), V(None, cvec_tr)], [V(None, cvn_tr[c])])

            def out_proj(slot, srcT, src_reads, tiles=range(NT)):
                for stt in tiles:
                    for dh in range(2):
                        bk = nbank()
                        for c in range(4):
                            fw.op("pe", lambda e, stt=stt, dh=dh, c=c, bk=bk: e.matmul(
                                bank(bk)[:, :], lhsT=srcT[:, c, stt * 128:(stt + 1) * 128],
                                rhs=wslot[slot][:, c * 1024 + dh * 512: c * 1024 + (dh + 1) * 512],
                                start=(c == 0), stop=(c == 3)),
                                [src_reads(c), V(None, wslot_tr[slot])], [V(None, b_tr[bk])], inc=(c == 3))
                        fw.op("dve", lambda e, stt=stt, dh=dh, bk=bk: e.tensor_tensor(
                            out=x_sb[:, stt, dh * 512:(dh + 1) * 512], in0=bank(bk)[:, :],
                            in1=x_sb[:, stt, dh * 512:(dh + 1) * 512], op=ALU.add),
                            [V(None, b_tr[bk]), V(None, x_tr[stt][dh])], [V(None, x_tr[stt][dh])])

            ckpt(3)
            for sblk in range(4):
                sl = slice(sblk * 512, (sblk + 1) * 512)
                bm, be, b3 = nbank(), nbank(), nbank()
                for c in range(4):
                    fw.op("pe", lambda e, c=c, bm=bm, sl=sl: e.matmul(bank(bm)[:, :], lhsT=onesd_b[:], rhs=cvn[:, c, sl],
                                                                start=(c == 0), stop=(c == 3)),
                          [CB, V(None, cvn_tr[c])], [V(None, b_tr[bm])], inc=(c == 3))
                for c in range(4):
                    qi = sqn["n"] % 2; sqn["n"] += 1
                    fw.op("act", lambda e, c=c, sl=sl, qi=qi: e.activation(out=sqb2[qi], in_=cvn[:, c, sl], func=AF.Square),
                          [V(None, cvn_tr[c])], [V(None, sqb2_tr[qi])])
                    fw.op("pe", lambda e, c=c, be=be, qi=qi: e.matmul(bank(be)[:, :], lhsT=onesd_b[:], rhs=sqb2[qi],
                                                                       start=(c == 0), stop=(c == 3)),
                          [CB, V(None, sqb2_tr[qi])], [V(None, b_tr[be])], inc=True)
                fw.op("dve", lambda e, bm=bm: e.tensor_tensor(out=t1, in0=bank(bm)[:, :], in1=bank(bm)[:, :], op=ALU.mult)
                      if False else e.tensor_copy(out=t1, in_=bank(bm)[:, :]), [V(None, b_tr[bm])], [T1])
                fw.op("dve", lambda e: e.tensor_tensor(out=t2, in0=t1, in1=t1, op=ALU.mult), [T1], [T2])
                fw.op("dve", lambda e, be=be: e.tensor_tensor(out=t2, in0=bank(be)[:, :], in1=t2, op=ALU.subtract),
                      [V(None, b_tr[be]), T2], [T2])
                fw.op("act", lambda e: e.activation(out=t2, in_=t2, func=AF.Ln, bias=EPS), [T2], [T2])
                fw.op("act", lambda e: e.activation(out=t2, in_=t2, func=AF.Exp, scale=-0.5), [T2], [T2])
                for c in range(4):
                    ZC = V(None, z_tr[c])
                    fw.op("dve", lambda e, c=c, sl=sl: e.tensor_tensor(out=zbuf[:, c, :], in0=cvn[:, c, sl], in1=t1, op=ALU.subtract),
                          [V(None, cvn_tr[c]), T1], [ZC])
                    fw.op("dve", lambda e, c=c: e.tensor_tensor(out=zbuf[:, c, :], in0=zbuf[:, c, :], in1=t2, op=ALU.mult),
                          [ZC, T2], [ZC])
                    fw.op("act", lambda e, c=c: e.activation(out=zbuf[:, c, :], in_=zbuf[:, c, :], func=AF.Silu,
                                                            bias=cvec_sb[:, 8 + c:9 + c], scale=cvec_sb[:, 4 + c:5 + c]),
                          [ZC, V(None, cvec_tr)], [ZC])
                    qi = sqn["n"] % 2; sqn["n"] += 1
                    fw.op("act", lambda e, c=c, qi=qi: e.activation(out=sqb2[qi], in_=zbuf[:, c, :], func=AF.Square),
                          [ZC], [V(None, sqb2_tr[qi])])
                    fw.op("pe", lambda e, c=c, b3=b3, qi=qi: e.matmul(bank(b3)[:, :], lhsT=onesd_b[:], rhs=sqb2[qi],
                                                                       start=(c == 0), stop=(c == 3)),
                          [CB, V(None, sqb2_tr[qi])], [V(None, b_tr[b3])], inc=True)
                fw.op("act", lambda e, b3=b3: e.activation(out=t2, in_=bank(b3)[:, :], func=AF.Ln, bias=EPS), [V(None, b_tr[b3])], [T2])
                fw.op("act", lambda e: e.activation(out=t2, in_=t2, func=AF.Exp, scale=-0.5), [T2], [T2])
                for c in range(4):
                    fw.op("dve", lambda e, c=c, sl=sl: e.scalar_tensor_tensor(
                        out=cvn[:, c, sl], in0=zbuf[:, c, :], scalar=cvec_sb[:, 12 + c:13 + c], in1=t2, op0=ALU.mult, op1=ALU.mult),
                        [V(None, z_tr[c]), V(None, cvec_tr), T2], [V(None, cvn_tr[c])])
                out_proj(s_wo2, cvn, lambda c: V(None, cvn_tr[c]), tiles=range(4 * sblk, 4 * sblk + 4))

            ckpt(4)
            ckpt(5)
            s_v = wload(win_piece(2))

            RS.new_phase()
            qT, qT_tr = RS.carve([128, 4, S], BF16)
            kT, kT_tr = RS.carve([128, 4, S], BF16)
            v1, v1_tr = RS.carve([128, NT, 8, 65], BF16)
            QT = V(None, qT_tr); KT = V(None, kT_tr); V1 = V(None, v1_tr)
            ebuf, ebuf_tr, ptb, ptb_tr = [], [], [], []
            for i in range(3):
                a_, t_ = RS.carve([128, 640], BF16); ebuf.append(a_); ebuf_tr.append(t_[0])
                a_, t_ = RS.carve([128, 640], BF16); ptb.append(a_); ptb_tr.append(t_[0])
            anb, anb_tr = RS.carve([128, 512], BF16)
            ANB = V(None, anb_tr)
            fw.op("dve", lambda e: e.memset(v1[:, :, :, 64:65], 1.0), [], [V1])

            proj_fm(s_q, lambda fc, sblk, bk: fw.op("act", lambda e: e.mul(
                out=qT[:, fc, sblk * 512:(sblk + 1) * 512], in_=bank(bk)[:, :], mul=0.125),
                [V(None, b_tr[bk])], [QT]))
            s_wo1 = wload(wout_piece(0))
            proj_fm(s_k, lambda fc, sblk, bk: fw.op("dve", lambda e: e.tensor_copy(
                out=kT[:, fc, sblk * 512:(sblk + 1) * 512], in_=bank(bk)[:, :]),
                [V(None, b_tr[bk])], [KT]))
            for stt in range(NT):
                bk = nbank()
                for dc in range(8):
                    fw.op("pe", lambda e, stt=stt, dc=dc, bk=bk: e.matmul(
                        bank(bk)[:, :], lhsT=hT[:, dc, stt * 128:(stt + 1) * 128], rhs=wslot[s_v][:, dc * 512:(dc + 1) * 512],
                        start=(dc == 0), stop=(dc == 7)),
                        [HT, V(None, wslot_tr[s_v])], [V(None, b_tr[bk])], inc=(dc == 7))
                copy(evac_eng(), v1[:, stt, :, 0:64], bank(bk)[:, :].rearrange("p (h d) -> p h d", d=64),
                     [V(None, b_tr[bk])], [V1])

            ckpt(6)
            RA.new_phase()
            anT, anT_tr = RA.carve([128, 4, S], BF16)
            ANT = V(None, anT_tr)
            ebias, ebias_tr = [], []
            for i in range(2):
                a_, t_ = RA.carve([128, 5, 640], BF16); ebias.append(a_); ebias_tr.append(t_[0])

            def load_ebias(h, l=l):
                i = h % 2
                fw.dma("pool", lambda e: e.dma_start(out=ebias[i], in_=eb_d[l * 8 + h].rearrange("p (t k) -> p t k", k=640)),
                       [], [V(None, ebias_tr[i])], f"ld_eb{i}")

            def exp_ebias(h):
                i = h % 2
                fw.op("act", lambda e: e.activation(out=ebias[i], in_=ebias[i], func=AF.Exp),
                      [V(None, ebias_tr[i])], [V(None, ebias_tr[i])])

            _ad = deps_of([wslot_tr[1], wslot_tr[2]])
            aout = RW.t[:, 6144:6144 + 8192].rearrange("p (t f) -> p t f", f=512)
            aout_tr = [Tr(_ad) for _ in range(NT)]
            RW.live += aout_tr
            load_ebias(0)
            load_ebias(1)
            exp_ebias(0)
            exp_ebias(1)
            pend_exp = None
            SA, SB_ = 0, 2
            step = 0
            pendq = []
            for h in range(8):
                hc, hp = h // 2, (h % 2) * 64
                EBv = V(None, ebias_tr[h % 2])
                for i in range(NT):
                    j0 = min(max(i - 2, 0), NT - 5)
                    ty = 0 if 2 <= i <= 13 else (1 if i == 0 else 2 if i == 1 else 3 if i == 14 else 4)
                    sbk = (step % 3) * 2
                    eb_i = step % 3
                    obk = 6 + step % 2
                    if i == 8 and pend_exp is not None:
                        exp_ebias(pend_exp); pend_exp = None
                    for m in range(5):
                        bk_ = sbk + (0 if m < 4 else 1)
                        col = (m % 4) * 128
                        fw.op("pe", lambda e, m=m, bk_=bk_, col=col, i=i, j0=j0, hc=hc, hp=hp: e.matmul(
                            bank(bk_)[:, col:col + 128], lhsT=kT[hp:hp + 64, hc, (j0 + m) * 128:(j0 + m + 1) * 128],
                            rhs=qT[hp:hp + 64, hc, i * 128:(i + 1) * 128], start=True, stop=True),
                            [KT, QT], [V(None, b_tr[bk_])], inc=(m == 3 or m == 4))
                    if len(pendq) >= 2:
                        pendq.pop(0)()
                    fw.op("act", lambda e, sbk=sbk, eb_i=eb_i: e.activation(out=ebuf[eb_i][:, 0:512], in_=bank(sbk)[:, :], func=AF.Exp),
                          [V(None, b_tr[sbk])], [V(None, ebuf_tr[eb_i])])
                    fw.op("act", lambda e, sbk=sbk, eb_i=eb_i: e.activation(out=ebuf[eb_i][:, 512:640], in_=bank(sbk + 1)[:, 0:128], func=AF.Exp),
                          [V(None, b_tr[sbk + 1])], [V(None, ebuf_tr[eb_i])])
                    fw.op("dve", lambda e, eb_i=eb_i, ty=ty, h=h: e.tensor_tensor(
                        out=ptb[eb_i], in0=ebuf[eb_i], in1=ebias[h % 2][:, ty, :], op=ALU.mult),
                        [V(None, ebuf_tr[eb_i]), EBv], [V(None, ptb_tr[eb_i])])

                    def pv(i=i, j0=j0, h=h, eb_i=eb_i, obk=obk):
                        for m in range(5):
                            fw.op("pe", lambda e, m=m: e.matmul(
                                bank(obk)[:, 0:65], lhsT=ptb[eb_i][:, m * 128:(m + 1) * 128], rhs=v1[:, j0 + m, h, :],
                                start=(m == 0), stop=(m == 4)),
                                [V(None, ptb_tr[eb_i]), V1], [V(None, b_tr[obk])], inc=(m == 4))
                        fw.op("dve", lambda e: e.reciprocal(out=sm_sb[:, eb_i:eb_i + 1], in_=bank(obk)[:, 64:65]),
                              [V(None, b_tr[obk])], [V(None, sm_tr)])
                        fw.op("dve", lambda e: e.tensor_scalar(out=aout[:, i, h * 64:(h + 1) * 64], in0=bank(obk)[:, 0:64],
                                                                scalar1=sm_sb[:, eb_i:eb_i + 1], scalar2=None, op0=ALU.mult),
                              [V(None, b_tr[obk]), V(None, sm_tr)], [V(None, aout_tr[i])])
                    pendq.append(pv)
                    step += 1
                if h + 2 < 8:
                    while pendq:
                        pendq.pop(0)()
                    load_ebias(h + 2)
                    pend_exp = h + 2
            while pendq:
                pendq.pop(0)()

            ckpt(7)
            for i, RSTD in norm_loop(512, lambda i: aout[:, i, :], lambda i: V(None, aout_tr[i])):
                fw.op("dve", lambda e, i=i: e.scalar_tensor_tensor(
                    out=anb, in0=aout[:, i, :], scalar=rstd_sb[:, i:i + 1], in1=aog_bc[:], op0=ALU.mult, op1=ALU.mult),
                    [V(None, aout_tr[i]), RSTD, V(None, aog_tr)], [ANB])
                bk = 6 + i % 2
                for c in range(4):
                    fw.op("pe", lambda e, c=c, bk=bk: e.transpose(
                        out=bankb(bk)[:, c * 128:(c + 1) * 128], in_=anb[:, c * 128:(c + 1) * 128], identity=ident_b[:]),
                        [ANB, CB], [V(None, b_tr[bk])], inc=(c == 3))
                fw.op("act", lambda e, i=i, bk=bk: e.copy(out=anT[:, :, i * 128:(i + 1) * 128],
                                                          in_=bankb(bk)[:, 0:512].rearrange("p (c s) -> p c s", s=128)),
                      [V(None, b_tr[bk])], [ANT])
            out_proj(s_wo1, anT, lambda c: ANT)

            ckpt(8)
            load_gain(n2g_d[l])
            RA.new_phase(); RS.new_phase(); RW.new_phase()
            h2, h2_tr = RA.carve([128, NT, D], BF16, ntr=NT)
            for stt, RSTD in norm_loop(D, lambda i: x_sb[:, i, :], lambda i: V(None, x_tr[i])):
                fw.op("dve", lambda e, stt=stt: e.scalar_tensor_tensor(
                    out=h2[:, stt, :], in0=x_sb[:, stt, :], scalar=rstd_sb[:, stt:stt + 1], in1=g_bc[:],
                    op0=ALU.mult, op1=ALU.mult),
                    [V(None, x_tr[stt]), RSTD, V(None, g_tr)], [V(None, h2_tr[stt])])

            fw.dma("sp", lambda e: e.dma_start(out=h2_d.rearrange("(t p) d -> p t d", p=128), in_=h2),
                   [V(None, h2_tr)], [V(None, H2D)], "st_h2")
            gp, gp_tr, up_, up_tr, dpf, dp_tr = [None] * 3, [None] * 3, [None] * 3, [None] * 3, [None] * 3, [None] * 3
            a_, t_ = RW.carve([128, 4096], BF16); gp[0] = a_; gp_tr[0] = t_[0]
            a_, t_ = RW.carve([128, 4096], BF16); up_[0] = a_; up_tr[0] = t_[0]
            a_, t_ = RW.carve([128, 4096], BF16); dpf[0] = a_; dp_tr[0] = t_[0]
            a_, t_ = RW.carve([128, 4096], BF16); gp[1] = a_; gp_tr[1] = t_[0]
            a_, t_ = RS.carve([128, 4096], BF16); up_[1] = a_; up_tr[1] = t_[0]
            NSL = 3
            NPC = FF // 512
            pieces = [(e_, p_) for e_ in range(NE) for p_ in range(NPC)]

            def issue_gu(k, l=l):
                if k >= len(pieces):
                    return
                e_, p_ = pieces[k]
                si = k % NSL
                fsl = slice(p_ * 512, (p_ + 1) * 512)
                fw.dma("pool", lambda e: e.dma_start(
                    out=gp[si].rearrange("p (c f) -> p c f", f=512),
                    in_=wg_d[l * NE + e_][:, fsl].rearrange("(c p) f -> p c f", p=128)), [], [V(None, gp_tr[si])], f"ld_e{si}_g")
                fw.dma("pool", lambda e: e.dma_start(
                    out=up_[si].rearrange("p (c f) -> p c f", f=512),
                    in_=wu_d[l * NE + e_][:, fsl].rearrange("(c p) f -> p c f", p=128)), [], [V(None, up_tr[si])], f"ld_e{si}_u")

            def issue_d(k, l=l):
                if k >= len(pieces):
                    return
                e_, p_ = pieces[k]
                si = k % NSL
                fw.dma("pool", lambda e: e.dma_start(
                    out=dpf[si].rearrange("p (c f) -> p c f", f=1024),
                    in_=wd_d[l * NE + e_][p_ * 512:(p_ + 1) * 512, :].rearrange("(c p) f -> p c f", p=128)),
                    [], [V(None, dp_tr[si])], f"ld_e{si}_d")

            issue_gu(0)
            issue_d(0)
            issue_gu(1)

            ckpt(9)
            h2T, h2T_tr = [], []
            h2T_off = (RS.off + 63) // 64 * 64
            for i in range(2):
                a_, t_ = RS.carve([128, 8, 128], BF16); h2T.append(a_); h2T_tr.append(t_[0])
            assert RS.off == h2T_off + 4096
            aff, aff_tr = RS.carve([128, NT, NE], F32); AFF = V(None, aff_tr)
            rs_, rs_tr = RS.carve([128, NT], F32); RSV = V(None, rs_tr)
            Mf, Mf_tr = RS.carve([128, NT, NE], F32); MF = V(None, Mf_tr)
            Mb, Mb_tr = RS.carve([128, NT, NE], BF16); MB = V(None, Mb_tr)
            mr, mr_tr = RS.carve([128, NT, NE], F32); MR = V(None, mr_tr)
            Gt, Gt_tr = RS.carve([128, NT, NE], F32); GT = V(None, Gt_tr)
            sel, sel_tr = RS.carve([128, NT, CAP], BF16); SEL = V(None, sel_tr)
            selT, selT_tr = [], []
            for i in range(2):
                a_, t_ = RS.carve([128, 2, S], BF16); selT.append(a_); selT_tr.append(t_[0])
            xsT, xsT_tr = RS.carve([128, 8, CAP], BF16); XST = V(None, xsT_tr)
            hid, hid_tr = [], []
            for i in range(2):
                a_, t_ = RS.carve([128, CAP], BF16); hid.append(a_); hid_tr.append(t_[0])
            sg, sg_tr = [], []
            for i in range(2):
                a_, t_ = RS.carve([128, CAP], F32); sg.append(a_); sg_tr.append(t_[0])
            eo_sb, eo_tr = RS.carve([128, 2, D], BF16); EO = V(None, eo_tr)
            stmp, stmp_tr = [], []
            for i in range(2):
                a_, t_ = RS.carve([128, 512], F32); stmp.append(a_); stmp_tr.append(t_[0])
            bis, bis_tr = RS.carve([128, 8], F32); BIS = V(None, bis_tr)
            gebf, gebf_tr = RS.carve([128, 8], BF16); GEB = V(None, gebf_tr)
            affT = selT[0][0:16, :, :].rearrange("p a s -> p (a s)").bitcast(F32)
            AFT = V(None, selT_tr[0])
            bjunk = selT[1][0:16, 0, :]
            BJ = V(None, selT_tr[1])
            maskT = selT[1][0:16, :, :].rearrange("p a s -> p (a s)").bitcast(F32)

            LB = 0
            for stt in range(NT):
                bk = 1 + stt % 2
                hb = stt % 2
                for dc in range(8):
                    fw.op("pe", lambda e, stt=stt, dc=dc, bk=bk: e.transpose(
                        out=bankb(bk)[:, dc * 128:(dc + 1) * 128], in_=h2[:, stt, dc * 128:(dc + 1) * 128], identity=ident_b[:]),
                        [V(None, h2_tr[stt]), CB], [V(None, b_tr[bk])], inc=(dc == 7))
                copy("act", h2T[hb], bankb(bk).rearrange("p (c s) -> p c s", s=128), [V(None, b_tr[bk])], [V(None, h2T_tr[hb])])
                for dc in range(8):
                    fw.op("pe", lambda e, stt=stt, dc=dc, hb=hb: e.matmul(
                        bank(LB)[:, stt * NE:(stt + 1) * NE], lhsT=h2T[hb][:, dc, :], rhs=wr_sb[:, dc, :],
                        start=(dc == 0), stop=(dc == 7)),
                        [V(None, h2T_tr[hb]), V(None, wr_tr)], [V(None, b_tr[LB])], inc=(dc == 7))
            fw.op("act", lambda e: e.activation(out=aff.rearrange("p t e -> p (t e)"), in_=bank(LB)[:, 0:NT * NE], func=AF.Exp),
                  [V(None, b_tr[LB])], [AFF])
            fw.op("dve", lambda e: e.tensor_reduce(out=rs_, in_=aff, axis=AX.X, op=ALU.add), [AFF], [RSV])
            fw.op("dve", lambda e: e.reciprocal(out=rs_, in_=rs_), [RSV], [RSV])
            fw.op("dve", lambda e: e.tensor_tensor(out=aff, in0=aff, in1=rs_.unsqueeze(2).to_broadcast([128, NT, NE]), op=ALU.mult),
                  [AFF, RSV], [AFF])
            for t in range(NT):
                bk = 4 + t // 4
                fw.op("pe", lambda e, t=t, bk=bk: e.transpose(
                    out=bank(bk)[0:16, (t % 4) * 128:(t % 4 + 1) * 128], in_=aff[:, t, :], identity=ident_f),
                    [AFF, CST], [V(None, b_tr[bk])], inc=(t % 4 == 3))
            for q4 in range(4):
                copy(evac_eng(), affT[:, q4 * 512:(q4 + 1) * 512], bank(4 + q4)[0:16, :], [V(None, b_tr[4 + q4])], [AFT])
            ckpt(10)
            affT8 = selT[0].rearrange("p a s -> p (a s)").bitcast(F32)
            junk8 = selT[1][:, 0, :]
            rep_tr = [Tr(deps_of([selT_tr[0]])) for _ in range(7)]
            for j_ in range(1, 8):
                fw.dma("sp", lambda e, j_=j_: e.dma_start(out=affT8[16 * j_:16 * j_ + 16, :], in_=affT8[0:16, :]),
                       [AFT], [V(None, rep_tr[j_ - 1])], f"ld_rep{j_}")
            REP = V(None, rep_tr)
            lo8, mid8, cnt8 = bis[:, 0:1], bis[:, 1:2], bis[:, 2:3]
            jvec = cst[:, 544:545]
            NPASS = 9
            fw.op("dve", lambda e: e.memset(bis[:, 0:1], 0.0), [], [BIS])
            fw.op("dve", lambda e: e.tensor_scalar(out=mid8, in0=jvec, scalar1=0.125, scalar2=None, op0=ALU.mult), [CST, BIS], [BIS])
            for ps in range(NPASS):
                stp = 8.0 ** -(ps + 1)
                fw.op("dve", lambda e: e.tensor_scalar(out=junk8, in0=affT8, scalar1=mid8, scalar2=None, op0=ALU.is_ge, op1=ALU.add,
                                                        accum_out=cnt8), [AFT, REP, BIS], [BJ, BIS])
                fw.op("dve", lambda e: e.tensor_single_scalar(out=gebf[:, 0:1], in_=cnt8, scalar=CAP - 0.5, op=ALU.is_ge), [BIS], [GEB])
                fw.op("pe", lambda e: e.matmul(bank(3)[:, 0:1], lhsT=modm_b[:], rhs=gebf[:, 0:1], start=True, stop=True),
                      [CB, GEB], [V(None, b_tr[3])])
                fw.op("dve", lambda e, stp=stp: e.scalar_tensor_tensor(out=lo8, in0=bank(3)[:, 0:1], scalar=stp, in1=lo8,
                                                                       op0=ALU.mult, op1=ALU.add), [V(None, b_tr[3]), BIS], [BIS])
                if ps + 1 < NPASS:
                    fw.op("dve", lambda e, stp=stp: e.scalar_tensor_tensor(out=mid8, in0=jvec, scalar=stp / 8.0, in1=lo8,
                                                                           op0=ALU.mult, op1=ALU.add), [CST, BIS], [BIS])
            lo = bis[0:16, 0:1]
            fw.op("dve", lambda e: e.tensor_scalar(out=maskT, in0=affT, scalar1=lo, scalar2=None, op0=ALU.is_ge),
                  [AFT, REP, BIS], [BJ, AFT])
            ckpt(11)
            for t in range(NT):
                fw.op("pe", lambda e, t=t: e.transpose(out=bank(1)[:, t * NE:(t + 1) * NE], in_=maskT[:, t * 128:(t + 1) * 128],
                                                       identity=cst[0:16, 0:16]),
                      [BJ, CST], [V(None, b_tr[1])], inc=(t == NT - 1))
            fw.op("dve", lambda e: e.tensor_copy(out=Mf.rearrange("p t e -> p (t e)"), in_=bank(1)[:, 0:NT * NE]), [V(None, b_tr[1])], [MF])
            fw.op("act", lambda e: e.copy(out=Mb.rearrange("p t e -> p (t e)"), in_=bank(1)[:, 0:NT * NE]), [V(None, b_tr[1])], [MB])
            for t in range(NT):
                for t2_ in range(t + 1):
                    fw.op("pe", lambda e, t=t, t2_=t2_: e.matmul(
                        bank(2)[:, t * NE:(t + 1) * NE], lhsT=(tri_b[:] if t2_ == t else ones_b[:]), rhs=Mb[:, t2_, :],
                        start=(t2_ == 0), stop=(t2_ == t)),
                        [CB, MB], [V(None, b_tr[2])], inc=(t2_ == t))
            fw.op("dve", lambda e: e.scalar_tensor_tensor(out=mr.rearrange("p t e -> p (t e)"), in0=bank(2)[:, 0:NT * NE], scalar=1.0,
                                                          in1=Mf.rearrange("p t e -> p (t e)"), op0=ALU.add, op1=ALU.mult),
                  [V(None, b_tr[2]), MF], [MR])
            fw.op("dve", lambda e: e.tensor_scalar(out=mr, in0=mr, scalar1=-1.0, scalar2=None, op0=ALU.add), [MR], [MR])
            fw.op("dve", lambda e: e.tensor_tensor(out=Gt, in0=aff, in1=Mf, op=ALU.mult), [AFF, MF], [GT])

            ckpt(12)
            pk = {"n": 0}

            def build_sel_tile(ex, t):
                fw.op("dve", lambda e: e.tensor_scalar(out=sel[:, t, :], in0=iota_c, scalar1=mr[:, t, ex:ex + 1],
                                                        scalar2=None, op0=ALU.is_equal),
                      [CST, MR], [SEL])

            _hd = deps_of(h2_tr)
            for j_, (lst, trl, si_) in enumerate(((dpf, dp_tr, 1), (gp, gp_tr, 2), (up_, up_tr, 2), (dpf, dp_tr, 2))):
                lst[si_] = RA.t[:, j_ * 4096:(j_ + 1) * 4096]
                trl[si_] = Tr(_hd)
                RA.live.append(trl[si_])
            issue_d(1)
            issue_gu(2)
            issue_d(2)
            _xd = deps_of(h2T_tr)
            xs_tm = RS.t[:, h2T_off // 2: h2T_off // 2 + 2048].rearrange("p (a d) -> p a d", d=D)
            xs_tr = [Tr(_xd), Tr(_xd)]
            RS.live += xs_tr

            def gather_a(ex):
                ib = ex % 2
                for ct in range(2):
                    for t in range(NT):
                        fw.op("pe", lambda e, t=t, ct=ct: e.matmul(
                            bank(6)[:, ct * 2:ct * 2 + 2], lhsT=sel[:, t, ct * 128:(ct + 1) * 128], rhs=tokc_b[:, t, :],
                            start=(t == 0), stop=(t == NT - 1)),
                            [SEL, CB], [V(None, b_tr[6])], inc=(t == NT - 1))
                fw.op("dve", lambda e: e.tensor_copy(out=idxf_sb[:, 0:4], in_=bank(6)[:, 0:4]), [V(None, b_tr[6])], [V(None, idxf_tr)])
                for ct in range(2):
                    fw.op("dve", lambda e, ct=ct: e.scalar_tensor_tensor(
                        out=idxf_sb[:, 4 + ct:5 + ct], in0=idxf_sb[:, ct * 2:ct * 2 + 1], scalar=128.0,
                        in1=idxf_sb[:, ct * 2 + 1:ct * 2 + 2], op0=ALU.mult, op1=ALU.add),
                        [V(None, idxf_tr)], [V(None, idxf_tr)])
                fw.op("dve", lambda e: e.tensor_copy(out=idx_sb[:, ib * 2:ib * 2 + 2], in_=idxf_sb[:, 4:6]),
                      [V(None, idxf_tr)], [V(None, idx_tr[ib])])
                for ct in range(2):
                    fw.dma("pool", lambda e, ct=ct: e.indirect_dma_start(
                        out=xs_tm[:, ct, :], out_offset=None, in_=h2_d[:, :],
                        in_offset=bass.IndirectOffsetOnAxis(ap=idx_sb[:, ib * 2 + ct:ib * 2 + ct + 1], axis=0)),
                        [V(None, idx_tr[ib]), V(None, H2D)], [V(None, xs_tr[ct])], f"ld_gx{ct}")

            def gather_b(ex):
                for ct in range(2):
                    bk = 6 + ct
                    for dc in range(8):
                        fw.op("pe", lambda e, ct=ct, dc=dc, bk=bk: e.transpose(
                            out=bankb(bk)[:, dc * 128:(dc + 1) * 128], in_=xs_tm[:, ct, dc * 128:(dc + 1) * 128], identity=ident_b[:]),
                            [V(None, xs_tr[ct]), CB], [V(None, b_tr[bk])], inc=(dc == 7))
                    copy(evac_eng(), xsT[:, :, ct * 128:(ct + 1) * 128], bankb(bk).rearrange("p (c s) -> p c s", s=128),
                         [V(None, b_tr[bk])], [XST])

            def make_selT(ex):
                sT = selT[ex % 2]; sT_tr = selT_tr[ex % 2]
                for ct in range(2):
                    for t in range(NT):
                        bk = 6 + t // 8
                        fw.op("pe", lambda e, t=t, ct=ct, bk=bk: e.transpose(
                            out=bankb(bk)[:, (t % 8) * 128:(t % 8 + 1) * 128], in_=sel[:, t, ct * 128:(ct + 1) * 128], identity=ident_b[:]),
                            [SEL, CB], [V(None, b_tr[bk])], inc=(t % 8 == 7))
                    for hh in range(2):
                        copy(evac_eng(), sT[:, ct, hh * 1024:(hh + 1) * 1024], bankb(6 + hh), [V(None, b_tr[6 + hh])], [V(None, sT_tr)])

            def eo_mm(fcg, si, hb, first, last):
                dbuf = dpf[si][:, (fcg // 2) * 2048:(fcg // 2 + 1) * 2048]
                dtr = dp_tr[si]
                for ct in range(2):
                    for dh in range(2):
                        bk = 2 + ct * 2 + dh
                        fw.op("pe", lambda e, ct=ct, dh=dh, bk=bk: e.matmul(
                            bank(bk)[:, :], lhsT=hid[hb][:, ct * 128:(ct + 1) * 128],
                            rhs=dbuf[:, (fcg % 2) * 1024 + dh * 512: (fcg % 2) * 1024 + (dh + 1) * 512],
                            start=first, stop=last),
                            [V(None, hid_tr[hb]), V(None, dtr)], [V(None, b_tr[bk])], inc=last or (ct == 1 and dh == 1))

            def ffn(ex, hook):
                pend_eo = None
                nchunk = 0
                for p_ in range(NPC):
                    kpiece = pk["n"]
                    si = kpiece % NSL
                    for fcg in range(4):
                        hb = nchunk % 2
                        for (wbuf, wtr, colo) in ((gp[si], gp_tr[si], 0), (up_[si], up_tr[si], 256)):
                            for dc in range(8):
                                fw.op("pe", lambda e, dc=dc, wbuf=wbuf, colo=colo, hb=hb, fcg=fcg: e.matmul(
                                    bank(hb)[:, colo:colo + 256],
                                    lhsT=wbuf[:, dc * 512 + fcg * 128: dc * 512 + (fcg + 1) * 128],
                                    rhs=xsT[:, dc, :], start=(dc == 0), stop=(dc == 7)),
                                    [V(None, wtr), XST], [V(None, b_tr[hb])], inc=(dc == 7))
                        if fcg == 3:
                            issue_gu(kpiece + NSL)
                        if pend_eo is not None:
                            pend_eo()
                        fw.op("act", lambda e, hb=hb: e.activation(out=sg[hb], in_=bank(hb)[:, 0:256], func=AF.Silu),
                              [V(None, b_tr[hb])], [V(None, sg_tr[hb])])
                        fw.op("dve", lambda e, hb=hb: e.tensor_tensor(out=hid[hb], in0=bank(hb)[:, 256:512], in1=sg[hb], op=ALU.mult),
                              [V(None, b_tr[hb]), V(None, sg_tr[hb])], [V(None, hid_tr[hb])])
                        hook(nchunk)
                        first = (nchunk == 0)
                        last = (nchunk == 4 * NPC - 1)

                        def mk(fcg=fcg, si=si, hb=hb, first=first, last=last, rel=(fcg == 3), kpiece=kpiece):
                            def f():
                                eo_mm(fcg, si, hb, first, last)
                                if rel:
                                    issue_d(kpiece + NSL)
                            return f
                        pend_eo = mk()
                        nchunk += 1
                    pk["n"] += 1
                pend_eo()

            def eo_evac():
                for ct in range(2):
                    for dh in range(2):
                        bk = 2 + ct * 2 + dh
                        copy(evac_eng(), eo_sb[:, ct, dh * 512:(dh + 1) * 512], bank(bk)[:, :], [V(None, b_tr[bk])], [EO])

            def scatter(ex):
                sT = selT[ex % 2]; sT_tr = selT_tr[ex % 2]
                for stt in range(NT):
                    for dh in range(2):
                        bk = (6, 7, 0, 1, 2, 3, 4, 5)[(stt * 2 + dh) % 8]
                        for ct in range(2):
                            fw.op("pe", lambda e, stt=stt, dh=dh, ct=ct, bk=bk: e.matmul(
                                bank(bk)[:, :], lhsT=sT[:, ct, stt * 128:(stt + 1) * 128], rhs=eo_sb[:, ct, dh * 512:(dh + 1) * 512],
                                start=(ct == 0), stop=(ct == 1)),
                                [V(None, sT_tr), EO], [V(None, b_tr[bk])], inc=(ct == 1))
                        XH = V(None, x_tr[stt][dh])
                        if dh == 0:
                            fw.op("dve", lambda e, stt=stt, dh=dh, bk=bk: e.scalar_tensor_tensor(
                                out=x_sb[:, stt, dh * 512:(dh + 1) * 512], in0=bank(bk)[:, :], scalar=Gt[:, stt, ex:ex + 1],
                                in1=x_sb[:, stt, dh * 512:(dh + 1) * 512], op0=ALU.mult, op1=ALU.add),
                                [V(None, b_tr[bk]), GT, XH], [XH])
                        else:
                            k_ = stt % 2
                            fw.op("act", lambda e, stt=stt, bk=bk, k_=k_: e.activation(
                                out=stmp[k_], in_=bank(bk)[:, :], func=AF.Copy, scale=Gt[:, stt, ex:ex + 1]),
                                [V(None, b_tr[bk]), GT], [V(None, stmp_tr[k_])])
                            fw.op("pool", lambda e, stt=stt, dh=dh, k_=k_: e.tensor_tensor(
                                out=x_sb[:, stt, dh * 512:(dh + 1) * 512], in0=x_sb[:, stt, dh * 512:(dh + 1) * 512],
                                in1=stmp[k_], op=ALU.add),
                                [V(None, stmp_tr[k_]), XH], [XH])

            for t in range(NT):
                build_sel_tile(0, t)
            gather_a(0)
            make_selT(0)
            gather_b(0)
            def mk_hook(nxt):
                def hook(n):
                    if nxt >= NE:
                        return
                    if n < 8:
                        build_sel_tile(nxt, 2 * n)
                        build_sel_tile(nxt, 2 * n + 1)
                    elif n == 8:
                        gather_a(nxt)
                return hook

            for ex in range(NE):
                nxt = ex + 1
                ffn(ex, mk_hook(nxt))
                eo_evac()
                if nxt < NE:
                    make_selT(nxt)
                scatter(ex)
                if nxt < NE:
                    gather_b(nxt)

          except _Stop:
            pass
        load_gain(fg_d[0])
        for stt, RSTD in norm_loop(D, lambda i: x_sb[:, i, :], lambda i: V(None, x_tr[i])):
            fw.op("dve", lambda e, stt=stt: e.scalar_tensor_tensor(
                out=x_sb[:, stt, :], in0=x_sb[:, stt, :], scalar=rstd_sb[:, stt:stt + 1], in1=g_bc[:], op0=ALU.mult, op1=ALU.mult),
                [V(None, x_tr[stt]), RSTD, V(None, g_tr)], [V(None, x_tr[stt])])
        for t4 in range(4):
            fw.dma("sp", lambda e, t4=t4: e.dma_start(
                out=out_d[t4 * 512:(t4 + 1) * 512, :].rearrange("(t p) d -> p t d", p=128),
                in_=x_sb[:, t4 * 4:(t4 + 1) * 4, :]),
                [V(None, xflat(t4 * 4, (t4 + 1) * 4))], [], f"st_o{t4}")
        fw.final_wait("sp", [f"st_o{t4}" for t4 in range(4)])
        fw.emit()
    return nc


def _ebias_table(rpb):
    L = rpb.shape[0]
    p = np.arange(128)
    ka, kc = p // 64, p % 64
    qa, qc = p // 64, p % 64
    out = np.empty((L, 8, 128, 5, 5, 128), np.float32)
    win_start = np.clip(qc - 8, 0, 48)
    colmask = (kc[:, None] >= win_start[None, :]) & (kc[:, None] < win_start[None, :] + 16)
    coloff = np.clip(kc[:, None] - qc[None, :] + 15, 0, 30)
    for ty, i in enumerate((2, 0, 1, 14, 15)):
        j0 = min(max(i - 2, 0), 11)
        for m in range(5):
            kr = 2 * (j0 + m) + ka
            qr = 2 * i + qa
            rs = np.clip(qr - 4, 0, 24)
            rowmask = (kr[:, None] >= rs[None, :]) & (kr[:, None] < rs[None, :] + 8)
            rowoff = np.clip(kr[:, None] - qr[None, :] + 7, 0, 14)
            vals = rpb[:, :, rowoff, coloff]
            out[:, :, :, ty, m, :] = np.where((rowmask & colmask)[None, None], vals, np.float32(NEG))
    return np.ascontiguousarray(out.reshape(L * 8, 128, 3200))


def _consts():
    c = np.zeros((128, 704), np.float32)
    c[:, 0:128] = np.eye(128, dtype=np.float32)
    k = np.arange(128)
    c[:, 128:256] = (k[:, None] < k[None, :]).astype(np.float32)
    c[:, 256:512] = np.arange(256, dtype=np.float32)[None, :]
    c[:, 512:544:2] = np.arange(16, dtype=np.float32)[None, :]
    c[:, 513:544:2] = k[:, None].astype(np.float32)
    c[:, 544] = (k // 16 + 1).astype(np.float32)
    c[:, 576:704] = ((k[:, None] % 16) == (k[None, :] % 16)).astype(np.float32)
    return c


def prep(inputs, nl=DEPTH):
    f = lambda a: np.ascontiguousarray(np.asarray(a, dtype=np.float32))
    conv_w = f(inputs["conv_w"])
    cw = conv_w.reshape(DEPTH, 31, 4, 128).transpose(0, 3, 2, 1).reshape(DEPTH, 128, 124)
    vecs = np.stack([f(inputs["conv_b"]), f(inputs["conv_ln_g"]), f(inputs["conv_ln_b"]), f(inputs["conv_out_g"])], 1)
    cvec = vecs.reshape(DEPTH, 4, 4, 128).transpose(0, 3, 1, 2).reshape(DEPTH, 128, 16)
    shared = {
        "norm1_g": f(inputs["norm1_g"]), "norm2_g": f(inputs["norm2_g"]), "final_g": f(inputs["final_g"]).reshape(1, D),
        "attn_out_g": f(inputs["attn_out_g"]), "w_in": f(inputs["w_in"]), "w_out": f(inputs["w_out"]),
        "w_router": f(inputs["w_router"]),
        "w_gate": f(inputs["w_gate"]).reshape(DEPTH * NE, D, FF), "w_up": f(inputs["w_up"]).reshape(DEPTH * NE, D, FF),
        "w_down": f(inputs["w_down"]).reshape(DEPTH * NE, FF, D),
        "cw": np.ascontiguousarray(cw), "cvec": np.ascontiguousarray(cvec),
        "ebias": _ebias_table(f(inputs["rpb"])), "consts": _consts(),
    }
    return shared


_NC_CACHE = {}


def kernel(**inputs):
    x = np.ascontiguousarray(np.asarray(inputs["x"], dtype=np.float32))
    B = x.shape[0]
    shared = prep(inputs)
    if DEPTH not in _NC_CACHE:
        _NC_CACHE[DEPTH] = build(DEPTH)
    nc = _NC_CACHE[DEPTH]
    in_maps = [dict(shared, x=x[b]) for b in range(B)]
    res = run_bass_kernel_spmd(nc, in_maps, core_ids=list(range(B)))
    return np.stack([res.results[b]["out"] for b in range(B)], 0).astype(np.float32)
```
